# Optimizing a Trainium2 kernel written in Bass

```python
import jax, jax.numpy as jnp
from jax import lax
import numpy as np

D_MODEL = 1024
BATCH = 8
SEQ = 2048
DEPTH = 1
DEC_BATCH = 128
DEC_SEQ = 1
PAST_LEN = 16384
PAGE_SIZE = 128

D_MIX = D_MODEL
D_S5 = D_MIX // 2
S5_GROUP = 16
S5_N_GROUPS = D_S5 // S5_GROUP
S5_STATE = 64
D_GDN = D_MIX - D_S5
GDN_HEAD_DIM = 128
GDN_HEADS = D_GDN // GDN_HEAD_DIM
GDN_CONV = 4
GDN_CHUNK = 64
D_FF = 2816
FFN_CONV = 3
NORM_EPS = 1e-6
D_IN = D_S5 + 4 * D_GDN + 2 * GDN_HEADS

kernel_name = 'hymba_s5_gdn_convffn_step'


def rmsnorm(x, g):
    xf = x.astype(jnp.float32)
    xf = xf * lax.rsqrt(jnp.mean(xf * xf, axis=-1, keepdims=True) + NORM_EPS)
    return (xf * g.astype(jnp.float32)).astype(x.dtype)


def l2norm(x):
    xf = x.astype(jnp.float32)
    return xf * lax.rsqrt(jnp.sum(xf * xf, axis=-1, keepdims=True) + NORM_EPS)


def causal_depthwise_conv(x, buf, w):
    width = w.shape[0]
    seq = x.shape[1]
    xp = jnp.concatenate([buf.astype(x.dtype), x], axis=1)
    y = xp[:, 0:seq] * w[0]
    for j in range(1, width):
        y = y + xp[:, j:j + seq] * w[j]
    return y, xp[:, seq:]


def s5_mixer(u, h0_re, h0_im, a_re, a_im, log_dt, b_re, b_im, c_re, c_im, d, w_glu):
    bsz, seq, _ = u.shape
    uf = u.astype(jnp.float32)
    ug = uf.reshape(bsz, seq, S5_N_GROUPS, S5_GROUP)
    ar = a_re.astype(jnp.float32)
    ai = a_im.astype(jnp.float32)
    dt = jnp.exp(log_dt.astype(jnp.float32))[:, None]
    mag = jnp.exp(ar * dt)
    ab_re = mag * jnp.cos(ai * dt)
    ab_im = mag * jnp.sin(ai * dt)
    den = ar * ar + ai * ai
    p = ab_re - 1.0
    f_re = (p * ar + ab_im * ai) / den
    f_im = (ab_im * ar - p * ai) / den
    bu_re = jnp.einsum('bsgc,gnc->bsgn', ug, b_re.astype(jnp.float32))
    bu_im = jnp.einsum('bsgc,gnc->bsgn', ug, b_im.astype(jnp.float32))
    x_re = f_re * bu_re - f_im * bu_im
    x_im = f_re * bu_im + f_im * bu_re
    h0r = h0_re.astype(jnp.float32)
    h0i = h0_im.astype(jnp.float32)
    x_re = x_re.at[:, 0].add(ab_re * h0r - ab_im * h0i)
    x_im = x_im.at[:, 0].add(ab_re * h0i + ab_im * h0r)
    a_full_re = jnp.broadcast_to(ab_re, x_re.shape)
    a_full_im = jnp.broadcast_to(ab_im, x_im.shape)

    def combine(e1, e2):
        a1r, a1i, b1r, b1i = e1
        a2r, a2i, b2r, b2i = e2
        return (a2r * a1r - a2i * a1i, a2r * a1i + a2i * a1r,
                a2r * b1r - a2i * b1i + b2r, a2r * b1i + a2i * b1r + b2i)

    _, _, h_re, h_im = lax.associative_scan(combine, (a_full_re, a_full_im, x_re, x_im), axis=1)
    y = (jnp.einsum('bsgn,gcn->bsgc', h_re, c_re.astype(jnp.float32))
         - jnp.einsum('bsgn,gcn->bsgc', h_im, c_im.astype(jnp.float32)))
    y = y.reshape(bsz, seq, D_S5) + d.astype(jnp.float32) * uf
    y = jax.nn.gelu(y)
    gl = y @ w_glu.astype(jnp.float32)
    out = gl[..., :D_S5] * jax.nn.sigmoid(gl[..., D_S5:])
    return out.astype(u.dtype), h_re[:, -1], h_im[:, -1]


def gdn_chunked(q, k, v, g, beta, s0):
    bsz, seq = q.shape[0], q.shape[1]
    nc = -(-seq // GDN_CHUNK)
    pad = nc * GDN_CHUNK - seq

    def prep(t):
        t = jnp.pad(t.astype(jnp.float32), [(0, 0), (0, pad)] + [(0, 0)] * (t.ndim - 2))
        t = t.reshape((bsz, nc, GDN_CHUNK) + t.shape[2:])
        return jnp.moveaxis(t, 3, 1)

    q, k, v, g, beta = prep(q), prep(k), prep(v), prep(g), prep(beta)
    dv = v.shape[-1]
    gc = jnp.cumsum(g, axis=-1)
    idx = jnp.arange(GDN_CHUNK)
    causal = idx[:, None] >= idx[None, :]
    strict = idx[:, None] > idx[None, :]
    decay = jnp.exp(jnp.where(causal, gc[..., :, None] - gc[..., None, :], -jnp.inf))
    kb = k * beta[..., None]
    vb = v * beta[..., None]
    lmat = jnp.where(strict, jnp.einsum('bhnid,bhnjd->bhnij', kb, k) * decay, 0.0)
    eye = jnp.eye(GDN_CHUNK, dtype=jnp.float32)
    rhs = jnp.concatenate([vb, kb * jnp.exp(gc)[..., None]], axis=-1)
    sol = lax.linalg.triangular_solve(eye + lmat, rhs, left_side=True, lower=True)
    u_c, w_c = sol[..., :dv], sol[..., dv:]
    attn = jnp.einsum('bhnid,bhnjd->bhnij', q, k) * decay
    qg = q * jnp.exp(gc)[..., None]
    kg = k * jnp.exp(gc[..., -1:] - gc)[..., None]
    glast = jnp.exp(gc[..., -1])
    xs = tuple(jnp.moveaxis(t, 2, 0) for t in (u_c, w_c, attn, qg, kg, glast))

    def step(s, inp):
        uu, ww, aa, qq, kk, gl = inp
        v_new = uu - jnp.einsum('bhcd,bhde->bhce', ww, s)
        o = jnp.einsum('bhcd,bhde->bhce', qq, s) + jnp.einsum('bhij,bhje->bhie', aa, v_new)
        s = s * gl[..., None, None] + jnp.einsum('bhcd,bhce->bhde', kk, v_new)
        return s, o

    s_fin, o = lax.scan(step, s0.astype(jnp.float32), xs)
    o = o.transpose(1, 0, 3, 2, 4).reshape(bsz, nc * GDN_CHUNK, GDN_HEADS, dv)[:, :seq]
    return o, s_fin


def gdn_mixer(qkv, z, beta_raw, a_raw, s0, conv_buf, conv_w, a_log, dt_bias, onorm_g):
    bsz, seq, _ = qkv.shape
    qkv_c, new_buf = causal_depthwise_conv(qkv, conv_buf, conv_w)
    qkv_c = jax.nn.silu(qkv_c)
    q = qkv_c[..., :D_GDN].reshape(bsz, seq, GDN_HEADS, GDN_HEAD_DIM)
    k = qkv_c[..., D_GDN:2 * D_GDN].reshape(bsz, seq, GDN_HEADS, GDN_HEAD_DIM)
    v = qkv_c[..., 2 * D_GDN:].reshape(bsz, seq, GDN_HEADS, GDN_HEAD_DIM)
    q = l2norm(q) * (GDN_HEAD_DIM ** -0.5)
    k = l2norm(k)
    beta = jax.nn.sigmoid(beta_raw.astype(jnp.float32))
    g = -jnp.exp(a_log.astype(jnp.float32)) * jax.nn.softplus(a_raw.astype(jnp.float32) + dt_bias.astype(jnp.float32))
    o, s_fin = gdn_chunked(q, k, v, g, beta, s0)
    zf = z.reshape(bsz, seq, GDN_HEADS, GDN_HEAD_DIM).astype(jnp.float32)
    o = rmsnorm(o, onorm_g) * jax.nn.silu(zf)
    return o.reshape(bsz, seq, D_GDN).astype(qkv.dtype), s_fin, new_buf


def conv_ffn(x, buf, w_up, conv_w, w_down):
    h = x @ w_up
    h, new_buf = causal_depthwise_conv(h, buf, conv_w)
    gate, up = h[..., :D_FF], h[..., D_FF:]
    return (jax.nn.silu(gate) * up) @ w_down, new_buf


def layer(x, s5_re, s5_im, gdn_s, gdn_buf, ffn_buf, norm1_g, w_in, s5_a_re, s5_a_im, s5_log_dt,
          s5_b_re, s5_b_im, s5_c_re, s5_c_im, s5_d, s5_w_glu, gdn_conv_w, gdn_a_log, gdn_dt_bias,
          gdn_onorm_g, w_out, norm2_g, ffn_w_up, ffn_conv_w, ffn_w_down):
    n = rmsnorm(x, norm1_g)
    proj = n @ w_in
    o0 = D_S5
    u_s5 = proj[..., :o0]
    qkv = proj[..., o0:o0 + 3 * D_GDN]
    z = proj[..., o0 + 3 * D_GDN:o0 + 4 * D_GDN]
    beta_raw = proj[..., o0 + 4 * D_GDN:o0 + 4 * D_GDN + GDN_HEADS]
    a_raw = proj[..., o0 + 4 * D_GDN + GDN_HEADS:]
    y_s5, s5_re, s5_im = s5_mixer(u_s5, s5_re, s5_im, s5_a_re, s5_a_im, s5_log_dt,
                                  s5_b_re, s5_b_im, s5_c_re, s5_c_im, s5_d, s5_w_glu)
    y_gdn, gdn_s, gdn_buf = gdn_mixer(qkv, z, beta_raw, a_raw, gdn_s, gdn_buf, gdn_conv_w,
                                      gdn_a_log, gdn_dt_bias, gdn_onorm_g)
    x = x + jnp.concatenate([y_s5, y_gdn], axis=-1) @ w_out
    y_ffn, ffn_buf = conv_ffn(rmsnorm(x, norm2_g), ffn_buf, ffn_w_up, ffn_conv_w, ffn_w_down)
    x = x + y_ffn
    return x, s5_re, s5_im, gdn_s, gdn_buf, ffn_buf


def setup_inputs(seed: int = 0) -> dict:
    key = jax.random.key(seed)
    ks = jax.random.split(key, 32)
    f32 = jnp.float32
    nrm = lambda k, shape, s: jax.random.normal(k, shape, f32) * s
    G, N = S5_N_GROUPS, S5_STATE
    a_im_base = jnp.pi * jnp.arange(N, dtype=f32)
    dt_g = jnp.exp(jax.random.uniform(ks[20], (DEPTH, GDN_HEADS), f32, np.log(1e-3), np.log(1e-1)))
    return {
        'x_prompt': nrm(ks[0], (BATCH, SEQ, D_MODEL), 1.0),
        'x_sample': nrm(ks[1], (DEC_BATCH, DEC_SEQ, D_MODEL), 1.0),
        'state_s5_re': nrm(ks[2], (DEPTH, DEC_BATCH, G, N), 0.5),
        'state_s5_im': nrm(ks[3], (DEPTH, DEC_BATCH, G, N), 0.5),
        'state_gdn': nrm(ks[4], (DEPTH, DEC_BATCH, GDN_HEADS, GDN_HEAD_DIM, GDN_HEAD_DIM), 0.3),
        'state_gdn_conv': nrm(ks[5], (DEPTH, DEC_BATCH, GDN_CONV - 1, 3 * D_GDN), 1.0),
        'state_ffn_conv': nrm(ks[6], (DEPTH, DEC_BATCH, FFN_CONV - 1, 2 * D_FF), 1.0),
        'norm1_g': 1.0 + nrm(ks[7], (DEPTH, D_MODEL), 0.02),
        'w_in': nrm(ks[8], (DEPTH, D_MODEL, D_IN), D_MODEL ** -0.5),
        's5_a_re': -0.5 + nrm(ks[9], (DEPTH, G, N), 0.01),
        's5_a_im': a_im_base + nrm(ks[10], (DEPTH, G, N), 0.01),
        's5_log_dt': jax.random.uniform(ks[11], (DEPTH, G), f32, np.log(1e-3), np.log(1e-1)),
        's5_b_re': nrm(ks[12], (DEPTH, G, N, S5_GROUP), (2 * S5_GROUP) ** -0.5),
        's5_b_im': nrm(ks[13], (DEPTH, G, N, S5_GROUP), (2 * S5_GROUP) ** -0.5),
        's5_c_re': nrm(ks[14], (DEPTH, G, S5_GROUP, N), N ** -0.5),
        's5_c_im': nrm(ks[15], (DEPTH, G, S5_GROUP, N), N ** -0.5),
        's5_d': nrm(ks[16], (DEPTH, D_S5), 0.5),
        's5_w_glu': nrm(ks[17], (DEPTH, D_S5, 2 * D_S5), D_S5 ** -0.5),
        'gdn_conv_w': nrm(ks[18], (DEPTH, GDN_CONV, 3 * D_GDN), GDN_CONV ** -0.5),
        'gdn_a_log': jnp.log(jax.random.uniform(ks[19], (DEPTH, GDN_HEADS), f32, 1.0, 16.0)),
        'gdn_dt_bias': dt_g + jnp.log(-jnp.expm1(-dt_g)),
        'gdn_onorm_g': 1.0 + nrm(ks[21], (DEPTH, GDN_HEAD_DIM), 0.02),
        'w_out': nrm(ks[22], (DEPTH, D_MIX, D_MODEL), D_MIX ** -0.5),
        'norm2_g': 1.0 + nrm(ks[23], (DEPTH, D_MODEL), 0.02),
        'ffn_w_up': nrm(ks[24], (DEPTH, D_MODEL, 2 * D_FF), D_MODEL ** -0.5),
        'ffn_conv_w': nrm(ks[25], (DEPTH, FFN_CONV, 2 * D_FF), FFN_CONV ** -0.5),
        'ffn_w_down': nrm(ks[26], (DEPTH, D_FF, D_MODEL), D_FF ** -0.5),
        'normf_g': 1.0 + nrm(ks[27], (D_MODEL,), 0.02),
    }


def reference(x_prompt, x_sample, state_s5_re, state_s5_im, state_gdn, state_gdn_conv, state_ffn_conv,
              norm1_g, w_in, s5_a_re, s5_a_im, s5_log_dt, s5_b_re, s5_b_im, s5_c_re, s5_c_im, s5_d,
              s5_w_glu, gdn_conv_w, gdn_a_log, gdn_dt_bias, gdn_onorm_g, w_out, norm2_g, ffn_w_up,
              ffn_conv_w, ffn_w_down, normf_g):
    f32 = jnp.float32
    xp, xs = x_prompt, x_sample
    p_new = ([], [], [], [], [])
    s_new = ([], [], [], [], [])
    for l in range(DEPTH):
        lw = (norm1_g[l], w_in[l], s5_a_re[l], s5_a_im[l], s5_log_dt[l], s5_b_re[l], s5_b_im[l],
              s5_c_re[l], s5_c_im[l], s5_d[l], s5_w_glu[l], gdn_conv_w[l], gdn_a_log[l], gdn_dt_bias[l],
              gdn_onorm_g[l], w_out[l], norm2_g[l], ffn_w_up[l], ffn_conv_w[l], ffn_w_down[l])
        outp = layer(xp,
                     jnp.zeros((BATCH, S5_N_GROUPS, S5_STATE), f32),
                     jnp.zeros((BATCH, S5_N_GROUPS, S5_STATE), f32),
                     jnp.zeros((BATCH, GDN_HEADS, GDN_HEAD_DIM, GDN_HEAD_DIM), f32),
                     jnp.zeros((BATCH, GDN_CONV - 1, 3 * D_GDN), xp.dtype),
                     jnp.zeros((BATCH, FFN_CONV - 1, 2 * D_FF), xp.dtype),
                     *lw)
        outs = layer(xs, state_s5_re[l], state_s5_im[l], state_gdn[l], state_gdn_conv[l],
                     state_ffn_conv[l], *lw)
        xp, xs = outp[0], outs[0]
        for i in range(5):
            p_new[i].append(outp[i + 1])
            s_new[i].append(outs[i + 1])
    y_prompt = rmsnorm(xp, normf_g)
    y_sample = rmsnorm(xs, normf_g)
    p_s5_re, p_s5_im, p_gdn, p_gdn_conv, p_ffn_conv = [jnp.stack(t, axis=0) for t in p_new]
    s_s5_re, s_s5_im, s_gdn, s_gdn_conv, s_ffn_conv = [jnp.stack(t, axis=0) for t in s_new]
    return (y_prompt, y_sample, p_s5_re, p_s5_im, p_gdn, p_gdn_conv, p_ffn_conv,
            s_s5_re, s_s5_im, s_gdn, s_gdn_conv, s_ffn_conv)
```

```python
import contextlib
import numpy as np
import concourse.bass as bass
import concourse.mybir as mybir
from concourse.bass_utils import run_bass_kernel_spmd

F32 = mybir.dt.float32
F32R = mybir.dt.float32r
ALU = mybir.AluOpType
AF = mybir.ActivationFunctionType

D = 1024
SEQ = 2048
NS_ = 16
TB = 512
DIN = 2568
DFF = 2816
NHT = 22
EPS = 1e-6
SW = 520
SB = 256
GELU_C = 0.7978845608028654
HGROUPS = [(0, 4), (4, 8), (8, 12), (12, 16), (16, 20), (20, 22)]


class Sched:
    def __init__(self, nc, es, ndma=16):
        self.nc = nc
        self.eng = {'pe': nc.tensor, 'act': nc.scalar, 'dve': nc.vector, 'pool': nc.gpsimd, 'sp': nc.sync}
        self.sem = {e: es.enter_context(nc.semaphore('s_' + e)) for e in self.eng}
        self.cnt = {e: 0 for e in self.eng}
        self.seen = {e: {} for e in self.eng}
        self.dsem = [es.enter_context(nc.semaphore('d%d' % i)) for i in range(ndma)]
        self.dcnt = [0] * ndma
        self.dnext = 0
        self.res = {}

    def _wait(self, e, dep):
        if dep is None:
            return
        key, val = dep
        if self.seen[e].get(key, 0) >= val:
            return
        self.seen[e][key] = val
        sem = self.sem[key] if isinstance(key, str) else self.dsem[key]
        self.eng[e].wait_ge(sem, val)

    def _deps(self, e, reads, writes):
        for t in reads:
            r = self.res.get(id(t))
            if r:
                self._wait(e, r['w'])
        for t in writes:
            r = self.res.get(id(t))
            if r:
                self._wait(e, r['w'])
                for d in r['r']:
                    self._wait(e, d)

    def _mark(self, tok, reads, writes):
        for t in reads:
            r = self.res.setdefault(id(t), {'w': None, 'r': []})
            r['r'] = [d for d in r['r'] if d[0] != tok[0]] + [tok]
        for t in writes:
            self.res[id(t)] = {'w': tok, 'r': []}

    def op(self, e, fn, reads=(), writes=()):
        writes = list(writes) + [t for t in reads if hasattr(t, 'idx')]
        reads = [t for t in reads if not hasattr(t, 'idx')]
        self._deps(e, reads, writes)
        ins = fn(self.eng[e])
        self.cnt[e] += 1
        ins.then_inc(self.sem[e], 1)
        tok = (e, self.cnt[e])
        if e == 'pe':
            self.seen[e][e] = self.cnt[e]
        self._mark(tok, reads, writes)
        return tok

    def dma(self, out, in_, reads=(), writes=(), q='sp', slow=False):
        i = self.dnext
        self.dnext = (self.dnext + 1) % len(self.dsem)
        if self.dcnt[i] > 0:
            self._wait(q, (i, self.dcnt[i]))
        self._deps(q, reads, writes)
        self.dcnt[i] += 16
        if slow:
            with self.nc.allow_non_contiguous_dma(reason="tiny per-partition element transfer"):
                self.eng[q].dma_start(out=out, in_=in_).then_inc(self.dsem[i], 16)
        else:
            self.eng[q].dma_start(out=out, in_=in_).then_inc(self.dsem[i], 16)
        tok = (i, self.dcnt[i])
        self._mark(tok, reads, writes)
        return tok

    def finish(self):
        for i in range(len(self.dsem)):
            if self.dcnt[i]:
                self._wait('sp', (i, self.dcnt[i]))
        for e in self.eng:
            if e != 'sp' and self.cnt[e]:
                self._wait('sp', (e, self.cnt[e]))


class Res:
    pass


def build():
    nc = bass.Bass("TRN2", target_bir_lowering=False)
    nc.dge_precook = False
    TT = SEQ + NS_

    def din(name, shape, dt=F32):
        return nc.dram_tensor(name, list(shape), dt, kind="ExternalInput").ap()

    def dout(name, shape):
        return nc.dram_tensor(name, list(shape), F32, kind="ExternalOutput").ap()

    xT = din("xT", [D, TT])
    w_in = din("w_in", [21, 128, 8, 128], F32R)
    w_glu = din("w_glu", [8, 128, 4, 128], F32R)
    w_out = din("w_out", [8, 128, 8, 128], F32R)
    w_up = din("w_up", [NHT, 128, 8, 256], F32R)
    w_down = din("w_down", [8, 128, NHT, 128], F32R)
    gains_d = din("gains", [128, 3, 8])
    s5bc_d = din("s5bc", [128, 3, 4, 64])
    s5bT_d = din("s5bT", [128, 2, 4, 64])
    s5ch_d = din("s5ch", [128, 3, 16])
    s5cT_d = din("s5cT", [128, 2, 16, 16])
    s5d_d = din("s5d", [128, 4])
    s5h0_d = din("s5h0", [128, 2, 16, 16])
    maskB_d = din("maskB", [128, 4, 128])
    maskC_d = din("maskC", [128, 4, 128])
    gcw_d = din("gcw", [128, 12, 4])
    gsm_d = din("gsm", [128, 9])
    onorm_d = din("onorm", [128, 1])
    gst_d = din("gst", [NS_, 4, 128, 128])
    gcs_d = din("gcs", [128, 12, 3, 16])
    fcw_d = din("fcw", [128, 44, 3])
    fcs_d = din("fcs", [128, 44, 2, 16])
    consts_d = din("consts", [128, 6, 128])
    iota_d = din("iota", [128, SB])

    yT = dout("yT", [D, TT])
    s5o_d = dout("s5o", [128, 2, 16, 17])
    gdno_d = dout("gdno", [17, 4, 128, 128])
    gco_d = dout("gco", [128, 12, 17, 3])
    fco_d = dout("fco", [128, 44, 2, 17])

    with contextlib.ExitStack() as es:
        S = Sched(nc, es)

        def sb(name, shape, dt=F32):
            return es.enter_context(nc.sbuf_tensor("t_" + name, list(shape), dt))

        es2 = contextlib.ExitStack()

        def sb2(name, shape, dt=F32):
            return es2.enter_context(nc.sbuf_tensor("t_" + name, list(shape), dt))

        consts = sb("consts", [128, 6, 128])
        ident = consts[:, 0, :]
        ones = consts[:, 1, :]
        tri2 = consts[:, 2, :]
        triC = consts[:, 3, :]
        mstrict = consts[:, 4, :]
        onesR = sb("onesR", [128, 128], F32R)
        identR = sb("identR", [128, 128], F32R)
        gains = sb("gains", [128, 3, 8])
        s5ch = sb("s5ch", [128, 3, 16])
        s5d = sb("s5d", [128, 4])
        s5h0 = sb("s5h0", [128, 2, 16, 16])
        Bst = sb("Bst", [128, 16, 2, 128], F32R)
        Cst = sb("Cst", [128, 16, 2, 128], F32R)
        cosT = sb("cosT", [128, 16, SB])
        sinT = sb("sinT", [128, 16, SB])
        mag = sb("mag", [128, 16])
        carry = sb("carry", [128, 2, 16])
        cth = sb("cth", [128, 16])
        sth = sb("sth", [128, 16])
        cu = sb("cu", [128, 16])
        su = sb("su", [128, 16])
        s5o = sb("s5o", [128, 2, 16, 17])
        gcw = sb("gcw", [128, 12, 4])
        gsm = sb("gsm", [128, 9])
        nega = sb("nega", [128, 4])
        onorm = sb("onorm", [128, 1])
        gtail = sb("gtail", [128, 12, 3])
        gcs = sb("gcs", [128, 12, 3, 16])
        gco = sb("gco", [128, 12, 17, 3])
        fcw = sb("fcw", [128, 44, 3])
        ftail = sb("ftail", [128, 44, 2])
        Sst = [sb("Sst%d" % h, [128, 128]) for h in range(4)]
        wring = [sb("wr%d" % i, [128, 8, 128], F32R) for i in range(3)]
        wuring = [sb("wu%d" % i, [128, 8, 256], F32R) for i in range(2)]
        wdring = [sb("wd%d" % i, [128, 4, 128], F32R) for i in range(3)]
        NG = 4
        gt = []
        for i in range(NG):
            d_ = {k: sb("g%s%d" % (k, i), [128, 128]) for k in
                  ("ek", "kg", "vtok", "gmat", "bmat", "A", "dTi", "dTs", "Ebc", "M0", "N0", "P0", "P1", "attnT", "qgT", "vnew")}
            d_["Pb"] = d_["gmat"]
            d_["uc"] = d_["bmat"]
            d_["wcT"] = d_["A"]
            d_["M1"] = d_["dTs"]
            d_["N1"] = d_["dTi"]
            gt.append(d_)
        gcol = sb("gcol", [128, 16, 4])
        s5bc = sb2("s5bc", [128, 3, 4, 64])
        s5bT = sb2("s5bT", [128, 2, 4, 64])
        s5cT = sb2("s5cT", [128, 2, 16, 16])
        maskB = sb2("maskB", [128, 4, 128])
        maskC = sb2("maskC", [128, 4, 128])
        iota = sb2("iota", [128, SB])
        psum = es.enter_context(nc.psum_tensor("psum", [128, 4096], F32))
        banks = [Res() for _ in range(8)]
        for i_, b_ in enumerate(banks):
            b_.idx = i_
        bfree = list(range(8))

        def bank(hold=False):
            i = bfree.pop(0)
            if not hold:
                bfree.append(i)
            return psum[:, i * 512:(i + 1) * 512], banks[i]

        def unhold(r):
            bfree.append(r.idx)

        free = []

        freeR = []
        rset = set()

        mins = {'f': 99, 'r': 99}

        def alloc():
            t = free.pop(0)
            mins['f'] = min(mins['f'], len(free))
            return t

        def allocR():
            t = freeR.pop(0)
            mins['r'] = min(mins['r'], len(freeR))
            return t

        def release(*ts):
            for t in ts:
                (freeR if id(t) in rset else free).append(t)

        def f(ap):
            return ap.bitcast(F32)

        def tt(e, out, a, b, op, R, W):
            S.op(e, lambda g: g.tensor_tensor(out=out, in0=a, in1=b, op=op), R, W)

        def stt(e, out, a, sc, b, op0, op1, R, W):
            S.op('dve', lambda g: g.scalar_tensor_tensor(out=out, in0=a, scalar=sc, in1=b, op0=op0, op1=op1), R, W)

        def ts(e, out, a, s1, s2, op0, op1, R, W):
            if s2 is None:
                S.op(e, lambda g: g.tensor_scalar(out=out, in0=a, scalar1=s1, scalar2=None, op0=op0), R, W)
            else:
                S.op(e, lambda g: g.tensor_scalar(out=out, in0=a, scalar1=s1, scalar2=s2, op0=op0, op1=op1), R, W)

        def act(out, in_, func, R, W, scale=1.0, bias=None):
            if bias is None:
                S.op('act', lambda g: g.activation(out=out, in_=in_, func=func, scale=scale), R, W)
            else:
                S.op('act', lambda g: g.activation(out=out, in_=in_, func=func, scale=scale, bias=bias), R, W)

        def cp(e, out, in_, R, W):
            if e == 'act':
                act(out, in_, AF.Copy, R, W)
            else:
                S.op(e, lambda g: g.tensor_copy(out=out, in_=in_), R, W)

        def mm(out, lhsT, rhs, R, W, start=True, stop=True):
            S.op('pe', lambda g: g.matmul(out, lhsT, rhs, start=start, stop=stop), R, W)

        def tr(out, in_, R, W):
            S.op('pe', lambda g: g.transpose(out, in_, ident), list(R) + [consts], W)

        for t, d in ((consts, consts_d), (gains, gains_d), (s5bc, s5bc_d), (s5bT, s5bT_d), (s5ch, s5ch_d),
                     (s5cT, s5cT_d), (s5d, s5d_d), (s5h0, s5h0_d), (maskB, maskB_d), (maskC, maskC_d),
                     (gcw, gcw_d), (gsm, gsm_d), (onorm, onorm_d), (gcs, gcs_d), (fcw, fcw_d),
                     (iota, iota_d)):
            S.dma(t[:], d, writes=[t])
        cp('pool', onesR[:], ones, [consts], [onesR])
        cp('pool', identR[:], ident, [consts], [identR])
        S.op('dve', lambda g: g.memset(gtail[:], 0.0), writes=[gtail])
        S.op('dve', lambda g: g.memset(ftail[:], 0.0), writes=[ftail])
        S.op('dve', lambda g: g.memset(carry[:], 0.0), writes=[carry])
        S.op('pool', lambda g: g.memset(s5o[:], 0.0), writes=[s5o])
        S.op('pool', lambda g: g.memset(gco[:], 0.0), writes=[gco])
        for h in range(4):
            S.op('pool', lambda g, h=h: g.memset(Sst[h][:], 0.0), writes=[Sst[h]])
        act(nega[:], gsm[:, 0:4], AF.Exp, [gsm], [nega])
        ts('dve', nega[:], nega[:], -1.0, None, ALU.mult, None, [nega], [nega])

        TWO_PI = 2.0 * np.pi

        I32 = mybir.dt.int32
        INV2PI = 1.0 / TWO_PI

        def mk_scr(shape, nm):
            return (sb2(nm + "y", shape), sb2(nm + "ki", shape, I32), sb2(nm + "kf", shape))

        def sin_turns(dst, src, shift, scr, R):
            y, ki, kf = scr
            ts('dve', y[:], src, shift, INV2PI, ALU.add, ALU.mult, R, [y])
            cp('dve', ki[:], y[:], [y], [ki])
            cp('dve', kf[:], ki[:], [ki], [kf])
            tt('dve', y[:], y[:], kf[:], ALU.subtract, [y, kf], [y])
            ts('dve', kf[:], y[:], 0.5, None, ALU.is_gt, None, [y], [kf])
            tt('dve', y[:], y[:], kf[:], ALU.subtract, [y, kf], [y])
            ts('dve', kf[:], y[:], -0.5, None, ALU.is_lt, None, [y], [kf])
            tt('dve', y[:], y[:], kf[:], ALU.add, [y, kf], [y])
            act(dst, y[:], AF.Sin, [y], R, scale=TWO_PI)

        def s5_abar(are, aim, ldt, shape, nm):
            t = {k: sb2(nm + k, shape) for k in ("dt", "mg", "ang", "sn", "cs", "den", "p", "t1", "t2", "fr", "fi")}
            R = [s5bc, s5ch]
            act(t["dt"][:], ldt, AF.Exp, R, [t["dt"]])
            tt('dve', t["mg"][:], are, t["dt"][:], ALU.mult, R + [t["dt"]], [t["mg"]])
            act(t["mg"][:], t["mg"][:], AF.Exp, [t["mg"]], [t["mg"]])
            tt('dve', t["ang"][:], aim, t["dt"][:], ALU.mult, R + [t["dt"]], [t["ang"]])
            scr = mk_scr(shape, nm + "s")
            sin_turns(t["sn"][:], t["ang"][:], 0.0, scr, [t["ang"], t["sn"]])
            sin_turns(t["cs"][:], t["ang"][:], 0.5 * np.pi, scr, [t["ang"], t["cs"]])
            tt('dve', t["cs"][:], t["cs"][:], t["mg"][:], ALU.mult, [t["cs"], t["mg"]], [t["cs"]])
            tt('dve', t["sn"][:], t["sn"][:], t["mg"][:], ALU.mult, [t["sn"], t["mg"]], [t["sn"]])
            tt('dve', t["den"][:], are, are, ALU.mult, R, [t["den"]])
            tt('dve', t["t1"][:], aim, aim, ALU.mult, R, [t["t1"]])
            tt('dve', t["den"][:], t["den"][:], t["t1"][:], ALU.add, [t["den"], t["t1"]], [t["den"]])
            S.op('dve', lambda g: g.reciprocal(out=t["den"][:], in_=t["den"][:]), [t["den"]], [t["den"]])
            ts('dve', t["p"][:], t["cs"][:], -1.0, None, ALU.add, None, [t["cs"]], [t["p"]])
            tt('dve', t["t1"][:], t["p"][:], are, ALU.mult, R + [t["p"]], [t["t1"]])
            tt('dve', t["t2"][:], t["sn"][:], aim, ALU.mult, R + [t["sn"]], [t["t2"]])
            tt('dve', t["fr"][:], t["t1"][:], t["t2"][:], ALU.add, [t["t1"], t["t2"]], [t["fr"]])
            tt('dve', t["fr"][:], t["fr"][:], t["den"][:], ALU.mult, [t["fr"], t["den"]], [t["fr"]])
            tt('dve', t["t1"][:], t["sn"][:], are, ALU.mult, R + [t["sn"]], [t["t1"]])
            tt('dve', t["t2"][:], t["p"][:], aim, ALU.mult, R + [t["p"]], [t["t2"]])
            tt('dve', t["fi"][:], t["t1"][:], t["t2"][:], ALU.subtract, [t["t1"], t["t2"]], [t["fi"]])
            tt('dve', t["fi"][:], t["fi"][:], t["den"][:], ALU.mult, [t["fi"], t["den"]], [t["fi"]])
            return t

        tb_ = s5_abar(s5bc[:, 0], s5bc[:, 1], s5bc[:, 2], [128, 4, 64], "sB")
        Btil = sb2("Btil", [128, 2, 4, 64])
        tmpB = sb2("tmpB", [128, 4, 64])
        RB = [s5bT, tb_["fr"], tb_["fi"]]
        tt('dve', Btil[:, 0], tb_["fr"][:], s5bT[:, 0], ALU.mult, RB, [Btil])
        tt('dve', tmpB[:], tb_["fi"][:], s5bT[:, 1], ALU.mult, RB, [tmpB])
        tt('dve', Btil[:, 0], Btil[:, 0], tmpB[:], ALU.subtract, [Btil, tmpB], [Btil])
        tt('dve', Btil[:, 1], tb_["fr"][:], s5bT[:, 1], ALU.mult, RB + [Btil], [Btil])
        tt('dve', tmpB[:], tb_["fi"][:], s5bT[:, 0], ALU.mult, RB, [tmpB])
        tt('dve', Btil[:, 1], Btil[:, 1], tmpB[:], ALU.add, [Btil, tmpB], [Btil])
        for j in range(16):
            for c in range(2):
                for hf in range(2):
                    tt('dve', Bst[:, j, c, hf * 64:(hf + 1) * 64], Btil[:, c, j // 4, :],
                       maskB[:, j % 4, hf * 64:(hf + 1) * 64], ALU.mult, [Btil, maskB], [Bst])
        tc_ = s5_abar(s5ch[:, 0], s5ch[:, 1], s5ch[:, 2], [128, 16], "sC")
        cp('dve', cth[:], tc_["cs"][:], [tc_["cs"]], [cth])
        cp('dve', sth[:], tc_["sn"][:], [tc_["sn"]], [sth])
        cp('dve', mag[:], tc_["mg"][:], [tc_["mg"]], [mag])
        rmag = sb2("rmag", [128, 16])
        S.op('dve', lambda g: g.reciprocal(out=rmag[:], in_=mag[:]), [mag], [rmag])
        tt('dve', cu[:], cth[:], rmag[:], ALU.mult, [cth, rmag], [cu])
        tt('dve', su[:], sth[:], rmag[:], ALU.mult, [sth, rmag], [su])
        for j in range(16):
            for c in range(2):
                for gq in range(8):
                    if c == 0:
                        tt('pool', Cst[:, j, c, gq * 16:(gq + 1) * 16], s5cT[:, c, j, :],
                           maskC[:, j % 4, gq * 16:(gq + 1) * 16], ALU.mult, [s5cT, maskC], [Cst])
                    else:
                        stt('pool', Cst[:, j, c, gq * 16:(gq + 1) * 16], s5cT[:, c, j, :], -1.0,
                            maskC[:, j % 4, gq * 16:(gq + 1) * 16], ALU.mult, ALU.mult, [s5cT, maskC], [Cst])
        thr = sb2("thr", [128, 16])
        thk = sb2("thk", [128, 16], I32)
        thf = sb2("thf", [128, 16])
        ts('dve', thr[:], tc_["ang"][:], INV2PI, None, ALU.mult, None, [tc_["ang"]], [thr])
        cp('dve', thk[:], thr[:], [thr], [thk])
        cp('dve', thf[:], thk[:], [thk], [thf])
        tt('dve', thr[:], thr[:], thf[:], ALU.subtract, [thr, thf], [thr])
        ts('dve', thr[:], thr[:], TWO_PI, None, ALU.mult, None, [thr], [thr])
        tmpT = sb2("tmpT", [128, 16, SB])
        for j in range(16):
            ts('dve', tmpT[:, j, :], iota[:], thr[:, j:j + 1], None, ALU.mult, None, [iota, thr], [tmpT])
        scr = mk_scr([128, 4, SB], "tb")
        for q in range(4):
            qs = slice(q * 4, q * 4 + 4)
            sin_turns(sinT[:, qs, :], tmpT[:, qs, :], 0.0, scr, [tmpT, sinT])
            sin_turns(cosT[:, qs, :], tmpT[:, qs, :], 0.5 * np.pi, scr, [tmpT, cosT])

        for e in ('pe', 'act', 'dve', 'pool', 'sp'):
            for e2 in ('pe', 'act', 'dve', 'pool'):
                if e2 != e and S.cnt[e2]:
                    S._wait(e, (e2, S.cnt[e2]))
            for i in range(len(S.dsem)):
                if S.dcnt[i]:
                    S._wait(e, (i, S.dcnt[i]))
        es2.close()
        rem = nc.sbuf_bytes_remaining
        nslots = min(int((rem - 1024) // (SW * 4)), 40)
        print("sbuf remaining", rem, "nslots", nslots)
        slots = [sb("slot%d" % i, [128, SW], F32R) for i in range(nslots)]
        NR = 19
        freeR.extend(slots[:NR])
        rset.update(id(t) for t in slots[:NR])
        free.extend(slots[NR:])

        wstate = {'i': 0, 'u': 0, 'd': 0}

        def load_w(src_ap, kt_n, ncols):
            t = wring[wstate['i'] % 3]
            wstate['i'] += 1
            S.dma(t[:, 0:kt_n, :], src_ap, writes=[t])
            return t

        def rmsnorm(src, gidx, n, out_r, dim=1024, inplace=False):
            kt_n = len(src)
            pb, pr = bank()
            for kt in range(kt_n):
                sq = allocR()
                act(sq[:, 0:n], f(src[kt][:, 0:n]), AF.Square, [src[kt]], [sq])
                mm(pb[:, 0:n], onesR[:], sq[:, 0:n], [sq, onesR], [pr], start=(kt == 0), stop=(kt == kt_n - 1))
                release(sq)
            rstd = alloc()
            ts('dve', f(rstd[:, 0:n]), pb[:, 0:n], 1.0 / dim, EPS, ALU.mult, ALU.add, [pr], [rstd])
            act(f(rstd[:, 0:n]), f(rstd[:, 0:n]), AF.Ln, [rstd], [rstd])
            act(f(rstd[:, 0:n]), f(rstd[:, 0:n]), AF.Exp, [rstd], [rstd], scale=-0.5)
            outs = []
            for kt in range(kt_n):
                o = src[kt] if inplace else (allocR() if out_r else alloc())
                dst = o[:, 0:n] if out_r else f(o[:, 0:n])
                stt('dve' if kt % 2 == 0 else 'pool', dst, f(src[kt][:, 0:n]), gains[:, gidx, kt:kt + 1],
                    f(rstd[:, 0:n]), ALU.mult, ALU.mult, [src[kt], rstd, gains], [o])
                outs.append(o)
            release(rstd)
            return outs

        import os
        KSTOP = float(os.environ.get("KSTOP", "999"))

        class _Stop(Exception):
            pass

        def chk(stage):
            if os.environ.get("KVERB"):
                print("chk", stage, dict(S.cnt), S.dcnt)
            if stage >= KSTOP:
                raise _Stop()

        blocks = [(b * TB, TB, 'p') for b in range(SEQ // TB)] + [(SEQ, NS_, 's')]
        try:
          chk(0)
          for bi, (c0, n, mode) in enumerate(blocks):
              last_p = (mode == 'p' and c0 + n == SEQ)
              smp = (mode == 's')
              xs = []
              for kt in range(8):
                  s_ = alloc()
                  S.dma(f(s_[:, 0:n]), xT[kt * 128:(kt + 1) * 128, c0:c0 + n], writes=[s_])
                  xs.append(s_)
              xn = rmsnorm(xs, 0, n, True)
              release(*xs)
              chk(1 + 10 * bi)

              def inproj(ct, ncols=128):
                  wt = load_w(w_in[ct], 8, ncols)
                  pb, pr = bank()
                  for kt in range(8):
                      mm(pb[0:ncols, 0:n], wt[:, kt, 0:ncols], xn[kt][:, 0:n], [wt, xn[kt]], [pr],
                         start=(kt == 0), stop=(kt == 7))
                  return pb, pr, wt

              us = []
              for ct in range(4):
                  pb, pr, _ = inproj(ct)
                  u = allocR()
                  cp('act', u[:, 0:n], pb[:, 0:n], [pr], [u])
                  us.append(u)
              ygs = []
              for yt in range(4):
                  ypb, ypr = bank(hold=True)
                  ycnt = {'k': 0}
                  def s5_gen(jj, yt=yt, ypb=ypb, ypr=ypr, ycnt=ycnt):
                      j = yt * 4 + jj
                      zr_b, zr_r = bank()
                      zi_b, zi_r = bank()
                      mm(zr_b[:, 0:n], Bst[:, j, 0, :], us[yt][:, 0:n], [Bst, us[yt]], [zr_r])
                      yield
                      mm(zi_b[:, 0:n], Bst[:, j, 1, :], us[yt][:, 0:n], [Bst, us[yt]], [zi_r])
                      yield
                      hr = allocR()
                      hi = allocR()
                      if smp:
                          t1 = alloc()
                          t2 = alloc()
                          stt('dve', f(t1[:, 0:n]), s5h0[:, 0, j, :], cth[:, j:j + 1], zr_b[:, 0:n], ALU.mult, ALU.add,
                              [s5h0, cth, zr_r], [t1])
                          yield
                          ts('dve', f(t2[:, 0:n]), s5h0[:, 1, j, :], sth[:, j:j + 1], None, ALU.mult, None, [s5h0, sth], [t2])
                          yield
                          tt('dve', hr[:, 0:n], f(t1[:, 0:n]), f(t2[:, 0:n]), ALU.subtract, [t1, t2], [hr])
                          yield
                          stt('dve', f(t1[:, 0:n]), s5h0[:, 1, j, :], cth[:, j:j + 1], zi_b[:, 0:n], ALU.mult, ALU.add,
                              [s5h0, cth, zi_r], [t1])
                          yield
                          stt('dve', hi[:, 0:n], s5h0[:, 0, j, :], sth[:, j:j + 1], f(t1[:, 0:n]), ALU.mult, ALU.add,
                              [s5h0, sth, t1], [hi])
                          yield
                          release(t1, t2)
                          cp('pool', s5o[:, 0, j, 1:17], f(hr[:, 0:n]), [hr], [s5o])
                          yield
                          cp('pool', s5o[:, 1, j, 1:17], f(hi[:, 0:n]), [hi], [s5o])
                          yield
                      else:
                          Zr = alloc()
                          Zi = alloc()
                          cp('act', f(Zr[:, 0:n]), zr_b[:, 0:n], [zr_r], [Zr])
                          yield
                          cp('act', f(Zi[:, 0:n]), zi_b[:, 0:n], [zi_r], [Zi])
                          yield
                          t1 = alloc()
                          t2 = alloc()
                          for sbk in range(n // SB):
                              cs = slice(sbk * SB, (sbk + 1) * SB)
                              cT = cosT[:, j, :]
                              sT = sinT[:, j, :]
                              RT = [cosT, sinT]
                              tt('dve', f(t1[:, cs]), f(Zr[:, cs]), cT, ALU.mult, RT + [Zr], [t1])
                              yield
                              tt('pool', f(t2[:, cs]), f(Zi[:, cs]), sT, ALU.mult, RT + [Zi], [t2])
                              yield
                              tt('dve', f(t1[:, cs]), f(t1[:, cs]), f(t2[:, cs]), ALU.add, [t1, t2], [t1])
                              yield
                              tt('pool', f(t2[:, cs]), f(Zi[:, cs]), cT, ALU.mult, RT + [Zi, t2], [t2])
                              yield
                              tt('dve', f(Zi[:, cs]), f(Zr[:, cs]), sT, ALU.mult, RT + [Zr], [Zi])
                              yield
                              tt('pool', f(t2[:, cs]), f(t2[:, cs]), f(Zi[:, cs]), ALU.subtract, [t2, Zi], [t2])
                              yield
                              S.op('dve', lambda g, cs=cs: g.tensor_tensor_scan(
                                  out=f(Zr[:, cs]), data0=mag[:, j:j + 1].to_broadcast([128, SB]), data1=f(t1[:, cs]),
                                  initial=carry[:, 0, j:j + 1], op0=ALU.mult, op1=ALU.add), [mag, t1, carry], [Zr])
                              yield
                              S.op('dve', lambda g, cs=cs: g.tensor_tensor_scan(
                                  out=f(Zi[:, cs]), data0=mag[:, j:j + 1].to_broadcast([128, SB]), data1=f(t2[:, cs]),
                                  initial=carry[:, 1, j:j + 1], op0=ALU.mult, op1=ALU.add), [mag, t2, carry], [Zi])
                              yield
                              tt('dve', f(t1[:, cs]), f(Zr[:, cs]), cT, ALU.mult, RT + [Zr], [t1])
                              yield
                              tt('pool', f(t2[:, cs]), f(Zi[:, cs]), sT, ALU.mult, RT + [Zi], [t2])
                              yield
                              tt('dve', hr[:, cs], f(t1[:, cs]), f(t2[:, cs]), ALU.subtract, [t1, t2], [hr])
                              yield
                              tt('pool', f(t2[:, cs]), f(Zi[:, cs]), cT, ALU.mult, RT + [Zi, t2], [t2])
                              yield
                              tt('dve', f(t1[:, cs]), f(Zr[:, cs]), sT, ALU.mult, RT + [Zr, t1], [t1])
                              yield
                              tt('pool', hi[:, cs], f(t2[:, cs]), f(t1[:, cs]), ALU.add, [t1, t2], [hi])
                              yield
                              e_ = (sbk + 1) * SB - 1
                              he_r = f(hr[:, e_:e_ + 1])
                              he_i = f(hi[:, e_:e_ + 1])
                              ts('dve', f(t1[:, 0:1]), he_i, su[:, j:j + 1], None, ALU.mult, None, [hi, su], [t1])
                              yield
                              stt('dve', carry[:, 0, j:j + 1], he_r, cu[:, j:j + 1], f(t1[:, 0:1]), ALU.mult, ALU.subtract,
                                  [hr, cu, t1], [carry])
                              yield
                              ts('dve', f(t1[:, 0:1]), he_r, su[:, j:j + 1], None, ALU.mult, None, [hr, su], [t1])
                              yield
                              stt('dve', carry[:, 1, j:j + 1], he_i, cu[:, j:j + 1], f(t1[:, 0:1]), ALU.mult, ALU.add,
                                  [hi, cu, t1], [carry])
                              yield
                          if last_p:
                              cp('pool', s5o[:, 0, j, 0:1], f(hr[:, n - 1:n]), [hr], [s5o])
                              yield
                              cp('pool', s5o[:, 1, j, 0:1], f(hi[:, n - 1:n]), [hi], [s5o])
                              yield
                          release(Zr, Zi, t1, t2)
                      mm(ypb[:, 0:n], Cst[:, j, 0, :], hr[:, 0:n], [Cst, hr], [ypr], start=(ycnt['k'] == 0), stop=False)
                      ycnt['k'] += 1
                      mm(ypb[:, 0:n], Cst[:, j, 1, :], hi[:, 0:n], [Cst, hi], [ypr], start=False, stop=(ycnt['k'] == 7))
                      ycnt['k'] += 1
                      release(hr, hi)
                  for jp in (0, 2):
                      gens = [s5_gen(jp), s5_gen(jp + 1)]
                      while gens:
                          for g_ in list(gens):
                              try:
                                  next(g_)
                              except StopIteration:
                                  gens.remove(g_)
                  y = alloc()
                  t1 = alloc()
                  stt('dve', f(y[:, 0:n]), f(us[yt][:, 0:n]), s5d[:, yt:yt + 1], ypb[:, 0:n], ALU.mult, ALU.add,
                      [us[yt], s5d, ypr], [y])
                  tt('pool', f(t1[:, 0:n]), f(y[:, 0:n]), f(y[:, 0:n]), ALU.mult, [y], [t1])
                  ts('dve', f(t1[:, 0:n]), f(t1[:, 0:n]), 0.044715, 1.0, ALU.mult, ALU.add, [t1], [t1])
                  tt('pool', f(t1[:, 0:n]), f(t1[:, 0:n]), f(y[:, 0:n]), ALU.mult, [t1, y], [t1])
                  act(f(t1[:, 0:n]), f(t1[:, 0:n]), AF.Tanh, [t1], [t1], scale=GELU_C)
                  stt('dve', f(t1[:, 0:n]), f(t1[:, 0:n]), 1.0, f(y[:, 0:n]), ALU.add, ALU.mult, [t1, y], [t1])
                  yg = allocR()
                  ts('dve', yg[:, 0:n], f(t1[:, 0:n]), 0.5, None, ALU.mult, None, [t1], [yg])
                  release(y, t1)
                  ygs.append(yg)
                  unhold(ypr)
              release(*us)
              chk(2 + 10 * bi)
              ymix = []
              for m in range(4):
                  wa = load_w(w_glu[m], 4, 128)
                  wb = load_w(w_glu[m + 4], 4, 128)
                  pa, par = bank()
                  pb, pbr = bank()
                  for kt in range(4):
                      mm(pa[:, 0:n], wa[:, kt, :], ygs[kt][:, 0:n], [wa, ygs[kt]], [par], start=(kt == 0), stop=(kt == 3))
                  for kt in range(4):
                      mm(pb[:, 0:n], wb[:, kt, :], ygs[kt][:, 0:n], [wb, ygs[kt]], [pbr], start=(kt == 0), stop=(kt == 3))
                  t1 = alloc()
                  act(f(t1[:, 0:n]), pb[:, 0:n], AF.Tanh, [pbr], [t1], scale=0.5)
                  ts('dve', f(t1[:, 0:n]), f(t1[:, 0:n]), 0.5, 0.5, ALU.mult, ALU.add, [t1], [t1])
                  o = allocR()
                  tt('dve', o[:, 0:n], f(t1[:, 0:n]), pa[:, 0:n], ALU.mult, [t1, par], [o])
                  release(t1)
                  ymix.append(o)
              release(*ygs)
              chk(3 + 10 * bi)

              pb, pr, wba = inproj(20, 8)
              ngrp = (n + 127) // 128
              bac_b, bac_r = bank(hold=True)
              for gi in range(ngrp):
                  gn = min(128, n - gi * 128)
                  for kt in range(8):
                      mm(bac_b[0:gn, gi * 8:gi * 8 + 8], xn[kt][:, gi * 128:gi * 128 + gn], wba[:, kt, 0:8],
                         [xn[kt], wba], [bac_r], start=(kt == 0), stop=(kt == 7))
              for gi in range(ngrp):
                  gn = min(128, n - gi * 128)
                  P_ = slice(0, gn)
                  R0 = [bac_r]
                  W0 = [gcol]
                  act(gcol[P_, gi, :], bac_b[P_, gi * 8:gi * 8 + 4], AF.Tanh, R0, W0, scale=0.5)
                  ts('dve', gcol[P_, 8 + gi, :], gcol[P_, gi, :], -0.5, -0.5, ALU.mult, ALU.add, [gcol], W0)
                  ts('dve', gcol[P_, gi, :], gcol[P_, gi, :], 0.5, 0.5, ALU.mult, ALU.add, [gcol], W0)
                  tt('dve', gcol[P_, 15, :], bac_b[P_, gi * 8 + 4:gi * 8 + 8], gsm[P_, 4:8], ALU.add, R0 + [gsm], W0)
                  act(gcol[P_, 12, :], gcol[P_, 15, :], AF.Abs, [gcol], W0)
                  act(gcol[P_, 12, :], gcol[P_, 12, :], AF.Exp, [gcol], W0, scale=-1.0)
                  act(gcol[P_, 12, :], gcol[P_, 12, :], AF.Ln, [gcol], W0, bias=1.0)
                  stt('dve', gcol[P_, 15, :], gcol[P_, 15, :], 0.0, gcol[P_, 12, :], ALU.max, ALU.add, [gcol], W0)
                  tt('dve', gcol[P_, 4 + gi, :], gcol[P_, 15, :], nega[P_, :], ALU.mult, [gcol, nega], W0)
              unhold(bac_r)
              chk(3.1 + 10 * bi)
              zts = []
              oTs = []
              def head_front(h):
                  qkv = []
                  for which in range(3):
                      ci = which * 4 + h
                      pb, pr, _ = inproj(4 + ci)
                      raw = alloc()
                      cp('act', f(raw[:, 4:4 + n]), pb[:, 0:n], [pr], [raw])
                      acc = alloc()
                      if smp:
                          ts('dve', f(acc[:, 0:n]), gcs[:, ci, 0, :], gcw[:, ci, 0:1], None, ALU.mult, None, [gcs, gcw], [acc])
                          for jx in (1, 2):
                              stt('dve', f(acc[:, 0:n]), gcs[:, ci, jx, :], gcw[:, ci, jx:jx + 1], f(acc[:, 0:n]),
                                  ALU.mult, ALU.add, [gcs, gcw, acc], [acc])
                          stt('dve', f(acc[:, 0:n]), f(raw[:, 4:4 + n]), gcw[:, ci, 3:4], f(acc[:, 0:n]),
                              ALU.mult, ALU.add, [raw, gcw, acc], [acc])
                          cp('pool', gco[:, ci, 1:17, 0], gcs[:, ci, 1, :], [gcs], [gco])
                          cp('pool', gco[:, ci, 1:17, 1], gcs[:, ci, 2, :], [gcs], [gco])
                          cp('pool', gco[:, ci, 1:17, 2], f(raw[:, 4:4 + n]), [raw], [gco])
                      else:
                          cp('pool', f(raw[:, 1:4]), gtail[:, ci, :], [gtail], [raw])
                          ts('dve', f(acc[:, 0:n]), f(raw[:, 1:1 + n]), gcw[:, ci, 0:1], None, ALU.mult, None, [raw, gcw], [acc])
                          for jx in (1, 2, 3):
                              stt('dve' if jx != 2 else 'pool', f(acc[:, 0:n]), f(raw[:, 1 + jx:1 + jx + n]),
                                  gcw[:, ci, jx:jx + 1], f(acc[:, 0:n]), ALU.mult, ALU.add, [raw, gcw, acc], [acc])
                          cp('pool', gtail[:, ci, :], f(raw[:, 1 + n:4 + n]), [raw], [gtail])
                          if last_p:
                              cp('pool', gco[:, ci, 0, :], f(raw[:, 1 + n:4 + n]), [raw], [gco])
                      act(f(raw[:, 0:n]), f(acc[:, 0:n]), AF.Tanh, [acc], [raw], scale=0.5)
                      ts('dve', f(raw[:, 0:n]), f(raw[:, 0:n]), 0.5, 0.5, ALU.mult, ALU.add, [raw], [raw])
                      tt('pool', f(acc[:, 0:n]), f(acc[:, 0:n]), f(raw[:, 0:n]), ALU.mult, [acc, raw], [acc])
                      if which < 2:
                          sqr = allocR()
                          act(sqr[:, 0:n], f(acc[:, 0:n]), AF.Square, [acc], [sqr])
                          nb, nr = bank()
                          mm(nb[:, 0:n], onesR[:], sqr[:, 0:n], [onesR, sqr], [nr])
                          release(sqr)
                          ts('dve', f(raw[:, 0:n]), nb[:, 0:n], EPS, None, ALU.add, None, [nr], [raw])
                          act(f(raw[:, 0:n]), f(raw[:, 0:n]), AF.Ln, [raw], [raw])
                          act(f(raw[:, 0:n]), f(raw[:, 0:n]), AF.Exp, [raw], [raw], scale=-0.5)
                          stt('dve', f(acc[:, 0:n]), f(acc[:, 0:n]), (128.0 ** -0.5) if which == 0 else 1.0,
                              f(raw[:, 0:n]), ALU.mult, ALU.mult, [acc, raw], [acc])
                      release(raw)
                      qkv.append(acc)
                  qT, kT, vT = qkv
                  chk(3.2 + 10 * bi)
                  pb, pr, _ = inproj(16 + h)
                  zt = alloc()
                  cp('act', f(zt[:, 0:n]), pb[:, 0:n], [pr], [zt])
                  chk(3.21 + 10 * bi)
                  ob, orr = bank(hold=True)
                  return dict(qT=qT, kT=kT, vT=vT, zt=zt, ob=ob, orr=orr)

              def head_sample(h, hd):
                  qT, kT, vT, ob, orr = hd['qT'], hd['kT'], hd['vT'], hd['ob'], hd['orr']
                  G = gt[0]
                  act(gcol[0:n, 13, :], gcol[0:n, 4, :], AF.Exp, [gcol], [gcol])
                  ts('dve', G["bmat"][0:n, :], ones[0:n, :], gcol[0:n, 8, h:h + 1], None, ALU.mult, None, [consts, gcol], [G["bmat"]])
                  ts('dve', G["gmat"][0:n, :], ones[0:n, :], gcol[0:n, 13, h:h + 1], None, ALU.mult, None, [consts, gcol], [G["gmat"]])
                  bb, bbr = bank()
                  mm(bb[:, 0:n], G["bmat"][0:n, :], ident[0:n, 0:n], [G["bmat"], consts], [bbr])
                  mm(bb[:, 64:64 + n], G["gmat"][0:n, :], ident[0:n, 0:n], [G["gmat"], consts], [bbr])
                  cp('act', G["Ebc"][:, 0:128], bb[:, 0:128], [bbr], [G["Ebc"]])
                  variants = []
                  for gg in gt:
                      variants.append(dict(S0=gg["M0"], vn=gg["vnew"], M1=gg["M1"], N0=gg["N0"], N1=gg["N1"]))
                      variants.append(dict(S0=gg["P0"], vn=gg["uc"], M1=gg["P1"], N0=gg["attnT"], N1=gg["qgT"]))
                      variants.append(dict(S0=gg["ek"], vn=gg["wcT"], M1=gg["kg"], N0=gg["vtok"], N1=gg["Pb"]))

                  def sample_gen(s, T):
                      S0 = T["S0"]
                      S.dma(S0[:], gst_d[s, h], writes=[S0])
                      yield
                      kb, kbr = bank()
                      mm(kb[:, 0:1], S0[:], f(kT[:, s:s + 1]), [S0, kT], [kbr])
                      yield
                      stt('dve', T["vn"][:, 0:1], kb[:, 0:1], G["Ebc"][:, 64 + s:65 + s], f(vT[:, s:s + 1]),
                          ALU.mult, ALU.subtract, [kbr, G["Ebc"], vT], [T["vn"]])
                      yield
                      ts('dve', T["vn"][:, 0:1], T["vn"][:, 0:1], G["Ebc"][:, s:s + 1], None, ALU.mult, None,
                         [T["vn"], G["Ebc"]], [T["vn"]])
                      yield
                      S.op('act', lambda g: g.activation(out=T["M1"][:], in_=ones, func=AF.Copy,
                           scale=T["vn"][:, 0:1]), [consts, T["vn"]], [T["M1"]])
                      yield
                      mm(kb[:, 128:256], T["M1"][:], ident, [T["M1"], consts], [kbr])
                      yield
                      ts('dve', T["N0"][:], kb[:, 128:256], f(kT[:, s:s + 1]), None, ALU.mult, None, [kbr, kT], [T["N0"]])
                      yield
                      stt('dve', T["N1"][:], S0[:], G["Ebc"][:, 64 + s:65 + s], T["N0"][:], ALU.mult, ALU.add,
                          [S0, G["Ebc"], T["N0"]], [T["N1"]])
                      yield
                      S.dma(gdno_d[1 + s, h], T["N1"][:], reads=[T["N1"]])
                      mm(ob[:, s:s + 1], T["N1"][:], f(qT[:, s:s + 1]), [T["N1"], qT], [orr])

                  nv = 6
                  for s0 in range(0, n, nv):
                      gens = [sample_gen(s0 + i, variants[i]) for i in range(min(nv, n - s0))]
                      while gens:
                          for g_ in list(gens):
                              try:
                                  next(g_)
                              except StopIteration:
                                  gens.remove(g_)

              def group_P(h, gi, G, hd):
                  qT, kT, vT, ob, orr = hd['qT'], hd['kT'], hd['vT'], hd['ob'], hd['orr']
                  gs = slice(gi * 128, (gi + 1) * 128)
                  cb, cbr = bank(hold=True)
                  mm(cb[:, 0:1], tri2, gcol[:, 4 + gi, h:h + 1], [consts, gcol], [cbr])
                  mm(cb[:, 1:2], triC, gcol[:, 4 + gi, h:h + 1], [consts, gcol], [cbr])
                  gck = Res()
                  cp('dve', G["vnew"][:, 0:1], cb[:, 0:1], [cbr], [G["vnew"]])
                  act(G["vnew"][:, 1:3], cb[:, 0:2], AF.Exp, [cbr], [G["vnew"]])
                  yield
                  chk(3.22 + 10 * bi)
                  unhold(cbr)
                  tb1, tbr = bank(hold=True)
                  tr(tb1[:, 0:128], f(kT[:, gs]), [kT], [tbr])
                  tr(tb1[:, 128:256], f(vT[:, gs]), [vT], [tbr])
                  yield
                  chk(3.225 + 10 * bi)
                  ts('dve', G["ek"][:], tb1[:, 0:128], G["vnew"][:, 1:2], None, ALU.mult, None, [tbr, G["vnew"]], [G["ek"]])
                  chk(3.226 + 10 * bi)
                  ts('dve', G["kg"][:], tb1[:, 0:128], G["vnew"][:, 2:3], None, ALU.mult, None, [tbr, G["vnew"]], [G["kg"]])
                  chk(3.227 + 10 * bi)
                  cp('dve', G["vtok"][:], tb1[:, 128:256], [tbr], [G["vtok"]])
                  yield
                  chk(3.23 + 10 * bi)
                  S.op('act', lambda g, G=G, gi=gi, h=h: g.activation(out=G["gmat"][:], in_=ones, func=AF.Copy,
                       scale=gcol[:, 4 + gi, h:h + 1]), [consts, gcol], [G["gmat"]])
                  S.op('act', lambda g, G=G, gi=gi, h=h: g.activation(out=G["bmat"][:], in_=ones, func=AF.Copy,
                       scale=gcol[:, gi, h:h + 1]), [consts, gcol], [G["bmat"]])
                  mm(tb1[:, 256:384], G["gmat"][:], tri2, [G["gmat"], consts], [tbr])
                  mm(tb1[:, 384:512], G["bmat"][:], ident, [G["bmat"], consts], [tbr])
                  yield
                  chk(3.24 + 10 * bi)
                  GCB = tb1[:, 256:384]
                  Bbc = tb1[:, 384:512]
                  ts('dve', G["A"][:], GCB, G["vnew"][:, 0:1], 0.0, ALU.subtract, ALU.min, [tbr, G["vnew"]], [G["A"]])
                  act(G["A"][:], G["A"][:], AF.Exp, [G["A"]], [G["A"]])
                  tt('pool', G["dTi"][:], G["A"][:], tri2, ALU.mult, [G["A"], consts], [G["dTi"]])
                  tt('pool', G["dTs"][:], G["A"][:], mstrict, ALU.mult, [G["A"], consts], [G["dTs"]])
                  act(G["Ebc"][:], GCB, AF.Exp, [tbr], [G["Ebc"]])
                  yield
                  chk(3.3 + 10 * bi)
                  kkb, kkr = bank(hold=True)
                  mm(kkb[:, 0:128], f(kT[:, gs]), f(kT[:, gs]), [kT], [kkr])
                  mm(kkb[:, 128:256], f(kT[:, gs]), f(qT[:, gs]), [kT, qT], [kkr])
                  yield
                  stt('dve', G["M0"][:], kkb[:, 0:128], gcol[:, 8 + gi, h:h + 1], G["dTs"][:], ALU.mult, ALU.mult,
                      [kkr, gcol, G["dTs"]], [G["M0"]])
                  tt('dve', G["attnT"][:], kkb[:, 128:256], G["dTi"][:], ALU.mult, [kkr, G["dTi"]], [G["attnT"]])
                  tt('pool', G["qgT"][:], f(qT[:, gs]), G["Ebc"][:], ALU.mult, [qT, G["Ebc"]], [G["qgT"]])
                  yield
                  tr(kkb[:, 256:384], G["M0"][:], [G["M0"]], [kkr])
                  yield
                  cp('dve', G["N0"][:], kkb[:, 256:384], [kkr], [G["N0"]])
                  unhold(kkr)
                  tt('pool', G["P0"][:], G["M0"][:], ident, ALU.add, [G["M0"], consts], [G["P0"]])
                  yield
                  chk(3.4 + 10 * bi)
                  Mc, Nc, Pc = "M0", "N0", "P0"
                  for lvl in range(5):
                      Mn, Nn, Pn = ("M1", "N1", "P1") if lvl % 2 == 0 else ("M0", "N0", "P0")
                      lb, lbr = bank(hold=True)
                      mm(lb[:, 0:128], G[Mc][:], G[Nc][:], [G[Mc], G[Nc]], [lbr])
                      if lvl < 4:
                          mm(lb[:, 128:256], G[Nc][:], G[Mc][:], [G[Mc], G[Nc]], [lbr])
                      cp('act', G[Nn][:], lb[:, 0:128], [lbr], [G[Nn]])
                      yield
                      if lvl < 4:
                          cp('dve', G[Mn][:], lb[:, 128:256], [lbr], [G[Mn]])
                      mm(lb[:, 256:384], G[Nn][:], G[Pc][:], [G[Nn], G[Pc]], [lbr])
                      yield
                      tt('dve', G[Pn][:], lb[:, 256:384], G[Pc][:], ALU.add, [lbr, G[Pc]], [G[Pn]])
                      unhold(lbr)
                      yield
                      Mc, Nc, Pc = Mn, Nn, Pn
                  tt('dve', G["Pb"][:], Bbc, G[Pc][:], ALU.mult, [tbr, G[Pc]], [G["Pb"]])
                  yield
                  chk(3.5 + 10 * bi)
                  unhold(tbr)
                  ub, ubr = bank(hold=True)
                  mm(ub[:, 0:128], G["Pb"][:], G["vtok"][:], [G["Pb"], G["vtok"]], [ubr])
                  mm(ub[:, 128:256], G["ek"][:], G["Pb"][:], [G["Pb"], G["ek"]], [ubr])
                  yield
                  cp('act', G["uc"][:], ub[:, 0:128], [ubr], [G["uc"]])
                  cp('dve', G["wcT"][:], ub[:, 128:256], [ubr], [G["wcT"]])
                  unhold(ubr)
                  yield
                  chk(3.6 + 10 * bi)

              def group_R(h, gi, G, hd):
                  qT, kT, vT, ob, orr = hd['qT'], hd['kT'], hd['vT'], hd['ob'], hd['orr']
                  for hf in range(2):
                      r = slice(hf * 64, hf * 64 + 64)
                      cg = slice(gi * 128 + hf * 64, gi * 128 + hf * 64 + 64)
                      wb_, wbr = bank(hold=True)
                      mm(wb_[:, 0:128], G["wcT"][:], Sst[h][:], [G["wcT"], Sst[h]], [wbr])
                      yield
                      tt('dve', G["vnew"][r, 0:128] if False else G["P0"][r, :], G["uc"][r, :], wb_[r, 0:128], ALU.subtract,
                         [G["uc"], wbr], [G["P0"]])
                      vn = G["P0"]
                      mm(ob[:, cg], Sst[h][:], G["qgT"][:, r], [Sst[h], G["qgT"]], [orr], start=True, stop=False)
                      mm(ob[:, cg], vn[r, :], G["attnT"][r, r], [vn, G["attnT"]], [orr], start=False, stop=True)
                      mm(wb_[:, 128:256], G["kg"][r, :], vn[r, :], [G["kg"], vn], [wbr])
                      yield
                      ec = hf * 64 + 63
                      stt('dve', Sst[h][:], Sst[h][:], G["Ebc"][:, ec:ec + 1], wb_[:, 128:256], ALU.mult, ALU.add,
                          [Sst[h], G["Ebc"], wbr], [Sst[h]])
                      unhold(wbr)

              def head_back(h, hd):
                  qT, kT, vT, zt, ob, orr = hd['qT'], hd['kT'], hd['vT'], hd['zt'], hd['ob'], hd['orr']
                  if last_p:
                      S.dma(gdno_d[0, h], Sst[h][:], reads=[Sst[h]])
                  chk(3.7 + 10 * bi)
                  release(qT, kT, vT)
                  oT = alloc()
                  cp('act', f(oT[:, 0:n]), ob[:, 0:n], [orr], [oT])
                  t1 = alloc()
                  sqr = allocR()
                  act(sqr[:, 0:n], f(oT[:, 0:n]), AF.Square, [oT], [sqr])
                  nb, nr = bank()
                  mm(nb[:, 0:n], onesR[:], sqr[:, 0:n], [onesR, sqr], [nr])
                  release(sqr)
                  ts('dve', f(t1[:, 0:n]), nb[:, 0:n], 1.0 / 128, EPS, ALU.mult, ALU.add, [nr], [t1])
                  act(f(t1[:, 0:n]), f(t1[:, 0:n]), AF.Ln, [t1], [t1])
                  act(f(t1[:, 0:n]), f(t1[:, 0:n]), AF.Exp, [t1], [t1], scale=-0.5)
                  stt('dve', f(oT[:, 0:n]), f(oT[:, 0:n]), onorm[:, 0:1], f(t1[:, 0:n]), ALU.mult, ALU.mult, [oT, onorm, t1], [oT])
                  act(f(t1[:, 0:n]), f(zt[:, 0:n]), AF.Tanh, [zt], [t1], scale=0.5)
                  stt('pool', f(t1[:, 0:n]), f(t1[:, 0:n]), 1.0, f(zt[:, 0:n]), ALU.add, ALU.mult, [t1, zt], [t1])
                  yo = allocR()
                  stt('dve', yo[:, 0:n], f(oT[:, 0:n]), 0.5, f(t1[:, 0:n]), ALU.mult, ALU.mult, [oT, t1], [yo])
                  release(oT, t1, zt)
                  unhold(orr)
                  ymix.append(yo)

              for hp in (0, 2):
                  hds = {h: head_front(h) for h in (hp, hp + 1)}
                  if smp:
                      for h in (hp, hp + 1):
                          head_sample(h, hds[h])
                  else:
                      def run_rr(gens):
                          while gens:
                              for g_ in list(gens):
                                  try:
                                      next(g_)
                                  except StopIteration:
                                      gens.remove(g_)

                      def gset(h, gi):
                          return gt[2 * (h % 2) + (gi % 2)]
                      run_rr([group_P(h, 0, gset(h, 0), hds[h]) for h in (hp, hp + 1)])
                      for gi in range(ngrp):
                          gens = [group_R(h, gi, gset(h, gi), hds[h]) for h in (hp, hp + 1)]
                          if gi + 1 < ngrp:
                              gens += [group_P(h, gi + 1, gset(h, gi + 1), hds[h]) for h in (hp, hp + 1)]
                          run_rr(gens)
                  for h in (hp, hp + 1):
                      head_back(h, hds[h])
              release(*xn)
              chk(4 + 10 * bi)

              x1 = []
              for m in range(8):
                  wt = load_w(w_out[m], 8, 128)
                  pb, pr = bank()
                  for kt in range(8):
                      mm(pb[:, 0:n], wt[:, kt, :], ymix[kt][:, 0:n], [wt, ymix[kt]], [pr], start=(kt == 0), stop=(kt == 7))
                  xr = alloc()
                  S.dma(f(xr[:, 0:n]), xT[m * 128:(m + 1) * 128, c0:c0 + n], writes=[xr])
                  tt('dve', f(xr[:, 0:n]), f(xr[:, 0:n]), pb[:, 0:n], ALU.add, [xr, pr], [xr])
                  x1.append(xr)
              release(*ymix)
              chk(5 + 10 * bi)
              xn2 = rmsnorm(x1, 1, n, True)
              for (h0, h1) in HGROUPS:
                  acts = []
                  acts_d = {}

                  def ffn_gen(j):
                      wu = wuring[wstate['u'] % 2]
                      wstate['u'] += 1
                      S.dma(wu[:], w_up[j], writes=[wu])
                      if smp:
                          sm = alloc()
                          for half in range(2):
                              for k in range(2):
                                  S.dma(f(sm[:, 128 + 32 * half + 16 * k:128 + 32 * half + 16 * k + n]), fcs_d[:, half * NHT + j, k, :], writes=[sm])
                      yield
                      cvd = []
                      pbs = []
                      for half in range(2):
                          pb, pr = bank(hold=True)
                          for kt in range(8):
                              mm(pb[:, 0:n], wu[:, kt, half * 128:(half + 1) * 128], xn2[kt][:, 0:n], [wu, xn2[kt]], [pr],
                                 start=(kt == 0), stop=(kt == 7))
                          pbs.append((pb, pr))
                      yield
                      rss = []
                      for half in range(2):
                          ci = half * NHT + j
                          pb, pr = pbs[half]
                          if smp:
                              raw_ap = f(sm[:, 4 + 200 * half:4 + 200 * half + n])
                              acc_ap = f(sm[:, 32 + 32 * half:32 + 32 * half + n])
                              fb0 = f(sm[:, 128 + 32 * half:128 + 32 * half + n])
                              fb1 = f(sm[:, 144 + 32 * half:144 + 32 * half + n])
                              cp('act', raw_ap, pb[:, 0:n], [pr], [sm])
                              e1 = 'dve'
                              ts(e1, acc_ap, fb0, fcw[:, ci, 0:1], None, ALU.mult, None, [fcw, sm], [sm])
                              stt(e1, acc_ap, fb1, fcw[:, ci, 1:2], acc_ap, ALU.mult, ALU.add,
                                  [fcw, sm], [sm])
                              stt(e1, acc_ap, raw_ap, fcw[:, ci, 2:3], acc_ap, ALU.mult, ALU.add,
                                  [fcw, sm], [sm])
                              S.dma(fco_d[:, ci, 0, 1:17], fb1, reads=[sm], q='act')
                              S.dma(fco_d[:, ci, 1, 1:17], raw_ap, reads=[sm], q='act')
                              cvd.append((acc_ap, sm, sm if half == 1 else None))
                              unhold(pr)
                          else:
                              rs = [allocR() for _ in range(3)]
                              for k in range(3):
                                  if k < 2:
                                      S.op('act', lambda g, k=k, rs=rs, pb=pb, ci=ci: g.activation(
                                          out=rs[k][:, 4:4 + n], in_=pb[:, 0:n], func=AF.Copy, scale=fcw[:, ci, k:k + 1]),
                                          [pr, fcw], [rs[k]])
                                  else:
                                      ts('dve', rs[k][:, 4:4 + n], pb[:, 0:n], fcw[:, ci, k:k + 1], None, ALU.mult, None, [pr, fcw], [rs[k]])
                              ts('dve', rs[0][:, 2:4], ftail[:, ci, 0:2], fcw[:, ci, 0:1], None, ALU.mult, None, [ftail, fcw], [rs[0]])
                              ts('dve', rs[1][:, 3:4], ftail[:, ci, 1:2], fcw[:, ci, 1:2], None, ALU.mult, None, [ftail, fcw], [rs[1]])
                              cp('dve', ftail[:, ci, :], pb[:, n - 2:n], [pr], [ftail])
                              if last_p:
                                  for k in range(2):
                                      S.dma(fco_d[:, ci, k, 0:1], ftail[:, ci, k:k + 1], reads=[ftail], q='act', slow=True)
                              unhold(pr)
                              rss.append(rs)
                      if not smp:
                          for half in range(2):
                              rs = rss[half]
                              cb, cr = bank(hold=True)
                              for k in range(3):
                                  mm(cb[:, 0:n], identR[:], rs[k][:, 2 + k:2 + k + n], [identR, rs[k]], [cr], start=(k == 0), stop=(k == 2))
                              release(*rs)
                              cvd.append((cb[:, 0:n], cr, None))
                      (g_ap, g_res, g_sl), (u_ap, u_res, u_sl) = cvd
                      if smp:
                          t1, t1_ap = sm, f(sm[:, 96:96 + n])
                      else:
                          t1 = allocR()
                          t1_ap = t1[:, 0:n]
                      act(t1_ap, g_ap, AF.Tanh, [g_res], [t1], scale=0.5)
                      yield
                      stt('dve', t1_ap, t1_ap, 1.0, g_ap, ALU.add, ALU.mult, [t1, g_res], [t1])
                      yield
                      a_ = allocR()
                      stt('dve', a_[:, 0:n], t1_ap, 0.5, u_ap, ALU.mult, ALU.mult, [t1, u_res], [a_])
                      if not smp:
                          release(t1)
                          unhold(g_res)
                          unhold(u_res)
                      if u_sl is not None:
                          release(u_sl)
                      acts_d[j] = a_

                  js = list(range(h0, h1))
                  step = 1
                  for q0 in range(0, len(js), step):
                      gens = [ffn_gen(j) for j in js[q0:q0 + step]]
                      while gens:
                          for g_ in list(gens):
                              try:
                                  next(g_)
                              except StopIteration:
                                  gens.remove(g_)
                  acts = [acts_d[j] for j in js]
                  nh = h1 - h0
                  for m in range(8):
                      wd = wdring[wstate['d'] % 3]
                      wstate['d'] += 1
                      S.dma(wd[:, 0:nh, :], w_down[m, :, h0:h1, :], writes=[wd])
                      pb, pr = bank()
                      for jj in range(nh):
                          mm(pb[:, 0:n], wd[:, jj, :], acts[jj][:, 0:n], [wd, acts[jj]], [pr], start=(jj == 0), stop=(jj == nh - 1))
                      tt('dve', f(x1[m][:, 0:n]), f(x1[m][:, 0:n]), pb[:, 0:n], ALU.add, [x1[m], pr], [x1[m]])
                  release(*acts)
              release(*xn2)
              chk(6 + 10 * bi)
              yf = rmsnorm(x1, 2, n, False, inplace=True)
              for m in range(8):
                  S.dma(yT[m * 128:(m + 1) * 128, c0:c0 + n], f(yf[m][:, 0:n]), reads=[yf[m]])
              release(*yf)


        except _Stop:
            pass
        print("min free slots", mins)
        S.dma(s5o_d, s5o[:], reads=[s5o])
        S.dma(gco_d, gco[:], reads=[gco])
        S.finish()
    return nc


_CACHE = {}


def _consts():
    idx = np.arange(128)
    same = (idx[:, None] // 64) == (idx[None, :] // 64)
    ident = np.eye(128, dtype=np.float32)
    ones = np.ones((128, 128), np.float32)
    tri2 = (same & (idx[:, None] <= idx[None, :])).astype(np.float32)
    triC = (same & (idx[:, None] > idx[None, :])).astype(np.float32)
    mstrict = (same & (idx[:, None] < idx[None, :])).astype(np.float32)
    c = np.stack([ident, ones, tri2, triC, mstrict, ones], axis=1)
    maskB = np.zeros((128, 4, 128), np.float32)
    maskC = np.zeros((128, 4, 128), np.float32)
    for m in range(4):
        for r in range(128):
            for col in range(128):
                if r // 16 == 2 * m + col // 64:
                    maskB[r, m, col] = 1.0
                if r // 64 + 2 * m == col // 16:
                    maskC[r, m, col] = 1.0
    iota = np.tile(np.arange(SB, dtype=np.float32)[None, :], (128, 1))
    return np.ascontiguousarray(c), maskB, maskC, iota


def kernel(**inp):
    f32 = np.float32
    A = lambda a: np.ascontiguousarray(np.asarray(a, dtype=f32))
    ncores = 8
    if 'nc' not in _CACHE:
        _CACHE['nc'] = build()
    nc = _CACHE['nc']
    consts, maskB, maskC, iota = _consts()
    xp = A(inp['x_prompt'])
    xs = A(inp['x_sample'])
    chl = lambda v: A(np.asarray(v).reshape(-1).reshape(-1, 128).T)
    gains = A(np.stack([chl(inp['norm1_g'][0]), chl(inp['norm2_g'][0]), chl(inp['normf_g'])], axis=1))
    are, aim, ldt = A(inp['s5_a_re'][0]), A(inp['s5_a_im'][0]), A(inp['s5_log_dt'][0])
    ldt_gn = np.repeat(ldt[:, None], 64, axis=1)

    def bside(v):
        v4 = v.reshape(4, 8, 64)
        v5 = np.repeat(v4[:, :, None, :], 16, axis=2)
        return v5.reshape(4, 128, 64).transpose(1, 0, 2)
    s5bc = A(np.stack([bside(are), bside(aim), bside(ldt_gn)], axis=1))

    def bT(v):
        v = np.asarray(v).transpose(0, 2, 1).reshape(4, 8, 16, 64)
        return v.reshape(4, 128, 64).transpose(1, 0, 2)
    s5bT = A(np.stack([bT(inp['s5_b_re'][0]), bT(inp['s5_b_im'][0])], axis=1))
    s5ch = A(np.stack([chl(are), chl(aim), chl(ldt_gn)], axis=1))

    def cT(v):
        v = np.asarray(v).transpose(0, 2, 1).reshape(16, 128, 16)
        return v.transpose(1, 0, 2)
    s5cT = A(np.stack([cT(inp['s5_c_re'][0]), cT(inp['s5_c_im'][0])], axis=1))
    s5d = chl(inp['s5_d'][0])
    gcw = A(np.asarray(inp['gdn_conv_w'][0]).T.reshape(12, 128, 4).transpose(1, 0, 2))
    gsm = np.zeros((128, 9), f32)
    gsm[:, 0:4] = np.asarray(inp['gdn_a_log'][0])[None, :]
    gsm[:, 4:8] = np.asarray(inp['gdn_dt_bias'][0])[None, :]
    onorm = A(np.asarray(inp['gdn_onorm_g'][0]).reshape(128, 1))
    fcw = A(np.asarray(inp['ffn_conv_w'][0]).T.reshape(44, 128, 3).transpose(1, 0, 2))
    def tile_w(w, ncol_tiles):
        K, C = w.shape
        wp = np.zeros((K, ncol_tiles * 128), f32)
        wp[:, :C] = w
        return A(wp.reshape(K // 128, 128, ncol_tiles, 128).transpose(2, 1, 0, 3))
    w_in = tile_w(np.asarray(inp['w_in'][0]), 21)
    w_glu = tile_w(np.asarray(inp['s5_w_glu'][0]), 8)
    w_out = tile_w(np.asarray(inp['w_out'][0]), 8)
    wu_ = np.asarray(inp['ffn_w_up'][0])
    w_up = A(np.concatenate([tile_w(wu_[:, :DFF], NHT), tile_w(wu_[:, DFF:], NHT)], axis=3))
    w_down = A(np.asarray(inp['ffn_w_down'][0]).reshape(NHT, 128, 8, 128).transpose(2, 1, 0, 3))
    sre, sim_ = np.asarray(inp['state_s5_re'][0]), np.asarray(inp['state_s5_im'][0])
    sg, sgc, sfc = np.asarray(inp['state_gdn'][0]), np.asarray(inp['state_gdn_conv'][0]), np.asarray(inp['state_ffn_conv'][0])
    in_maps = []
    for c in range(ncores):
        sl = slice(c * NS_, (c + 1) * NS_)
        xT = np.concatenate([xp[c].T, xs[sl, 0].T], axis=1)

        def h0(v):
            return v[sl].reshape(NS_, 16, 128).transpose(2, 1, 0)
        m = dict(xT=A(xT), w_in=w_in, w_glu=w_glu, w_out=w_out, w_up=w_up, w_down=w_down, gains=gains,
                 s5bc=s5bc, s5bT=s5bT, s5ch=s5ch, s5cT=s5cT, s5d=s5d,
                 s5h0=A(np.stack([h0(sre), h0(sim_)], axis=1)), maskB=maskB, maskC=maskC, gcw=gcw, gsm=gsm,
                 onorm=onorm, gst=A(sg[sl]),
                 gcs=A(sgc[sl].transpose(2, 1, 0).reshape(12, 128, 3, NS_).transpose(1, 0, 2, 3)),
                 fcw=fcw, fcs=A(sfc[sl].transpose(2, 1, 0).reshape(44, 128, 2, NS_).transpose(1, 0, 2, 3)),
                 consts=consts, iota=iota)
        in_maps.append(m)
    res = run_bass_kernel_spmd(nc, in_maps, core_ids=list(range(ncores)))
    R = res.results
    y_p = np.stack([R[c]['yT'][:, :SEQ].T for c in range(ncores)], axis=0)
    y_s = np.concatenate([R[c]['yT'][:, SEQ:].T for c in range(ncores)], axis=0)[:, None, :]

    def s5st(c, ri, lo, hi):
        v = R[c]['s5o'][:, ri, :, lo:hi]
        return v.transpose(2, 1, 0).reshape(hi - lo, 32, 64)
    p_re = np.concatenate([s5st(c, 0, 0, 1) for c in range(ncores)], 0)[None]
    p_im = np.concatenate([s5st(c, 1, 0, 1) for c in range(ncores)], 0)[None]
    s_re = np.concatenate([s5st(c, 0, 1, 17) for c in range(ncores)], 0)[None]
    s_im = np.concatenate([s5st(c, 1, 1, 17) for c in range(ncores)], 0)[None]
    p_g = np.stack([R[c]['gdno'][0] for c in range(ncores)], 0)[None]
    s_g = np.concatenate([R[c]['gdno'][1:] for c in range(ncores)], 0)[None]

    def cv(c, key, nt, w, lo, hi):
        v = R[c][key][:, :, lo:hi, :]
        return v.transpose(2, 3, 1, 0).reshape(hi - lo, w, nt * 128)
    p_gc = np.concatenate([cv(c, 'gco', 12, 3, 0, 1) for c in range(ncores)], 0)[None]
    s_gc = np.concatenate([cv(c, 'gco', 12, 3, 1, 17) for c in range(ncores)], 0)[None]
    def cvf(c, lo, hi):
        v = R[c]['fco'][:, :, :, lo:hi]
        return v.transpose(3, 2, 1, 0).reshape(hi - lo, 2, 44 * 128)
    p_fc = np.concatenate([cvf(c, 0, 1) for c in range(ncores)], 0)[None]
    s_fc = np.concatenate([cvf(c, 1, 17) for c in range(ncores)], 0)[None]
    outs = (y_p, y_s, p_re, p_im, p_g, p_gc, p_fc, s_re, s_im, s_g, s_gc, s_fc)
    return tuple(np.ascontiguousarray(o.astype(f32)) for o in outs)
```

```python
import contextlib
import numpy as np
import concourse.bass as bass
import concourse.mybir as mybir
from concourse.bass_utils import run_bass_kernel_spmd

F32 = mybir.dt.float32
F32R = mybir.dt.float32r
ALU = mybir.AluOpType
AF = mybir.ActivationFunctionType

D = 1024
SEQ = 2048
NS_ = 16
TB = 512
DIN = 2568
DFF = 2816
NHT = 22
EPS = 1e-6
SW = 520
SB = 256
GELU_C = 0.7978845608028654
HGROUPS = [(0, 4), (4, 8), (8, 12), (12, 16), (16, 20), (20, 22)]


class Sched:
    def __init__(self, nc, es, ndma=16):
        self.nc = nc
        self.eng = {'pe': nc.tensor, 'act': nc.scalar, 'dve': nc.vector, 'pool': nc.gpsimd, 'sp': nc.sync}
        self.sem = {e: es.enter_context(nc.semaphore('s_' + e)) for e in self.eng}
        self.cnt = {e: 0 for e in self.eng}
        self.seen = {e: {} for e in self.eng}
        self.dsem = [es.enter_context(nc.semaphore('d%d' % i)) for i in range(ndma)]
        self.dcnt = [0] * ndma
        self.dnext = 0
        self.res = {}

    def _wait(self, e, dep):
        if dep is None:
            return
        key, val = dep
        if self.seen[e].get(key, 0) >= val:
            return
        self.seen[e][key] = val
        sem = self.sem[key] if isinstance(key, str) else self.dsem[key]
        self.eng[e].wait_ge(sem, val)

    def _deps(self, e, reads, writes):
        for t in reads:
            r = self.res.get(id(t))
            if r:
                self._wait(e, r['w'])
        for t in writes:
            r = self.res.get(id(t))
            if r:
                self._wait(e, r['w'])
                for d in r['r']:
                    self._wait(e, d)

    def _mark(self, tok, reads, writes):
        for t in reads:
            r = self.res.setdefault(id(t), {'w': None, 'r': []})
            r['r'] = [d for d in r['r'] if d[0] != tok[0]] + [tok]
        for t in writes:
            self.res[id(t)] = {'w': tok, 'r': []}

    def op(self, e, fn, reads=(), writes=()):
        writes = list(writes) + [t for t in reads if hasattr(t, 'idx')]
        reads = [t for t in reads if not hasattr(t, 'idx')]
        self._deps(e, reads, writes)
        ins = fn(self.eng[e])
        self.cnt[e] += 1
        ins.then_inc(self.sem[e], 1)
        tok = (e, self.cnt[e])
        if e == 'pe':
            self.seen[e][e] = self.cnt[e]
        self._mark(tok, reads, writes)
        return tok

    def dma(self, out, in_, reads=(), writes=(), q='sp'):
        i = self.dnext
        self.dnext = (self.dnext + 1) % len(self.dsem)
        if self.dcnt[i] > 0:
            self._wait(q, (i, self.dcnt[i]))
        self._deps(q, reads, writes)
        self.dcnt[i] += 16
        self.eng[q].dma_start(out=out, in_=in_).then_inc(self.dsem[i], 16)
        tok = (i, self.dcnt[i])
        self._mark(tok, reads, writes)
        return tok

    def finish(self):
        for i in range(len(self.dsem)):
            if self.dcnt[i]:
                self._wait('sp', (i, self.dcnt[i]))
        for e in self.eng:
            if e != 'sp' and self.cnt[e]:
                self._wait('sp', (e, self.cnt[e]))


class Res:
    pass


def build():
    nc = bass.Bass("TRN2", target_bir_lowering=False)
    nc.dge_precook = False
    TT = SEQ + NS_

    def din(name, shape, dt=F32):
        return nc.dram_tensor(name, list(shape), dt, kind="ExternalInput").ap()

    def dout(name, shape):
        return nc.dram_tensor(name, list(shape), F32, kind="ExternalOutput").ap()

    xT = din("xT", [D, TT])
    w_in = din("w_in", [21, 128, 8, 128], F32R)
    w_glu = din("w_glu", [8, 128, 4, 128], F32R)
    w_out = din("w_out", [8, 128, 8, 128], F32R)
    w_up = din("w_up", [2 * NHT, 128, 8, 128], F32R)
    w_down = din("w_down", [8, 128, NHT, 128], F32R)
    gains_d = din("gains", [128, 3, 8])
    s5bc_d = din("s5bc", [128, 3, 4, 64])
    s5bT_d = din("s5bT", [128, 2, 4, 64])
    s5ch_d = din("s5ch", [128, 3, 16])
    s5cT_d = din("s5cT", [128, 2, 16, 16])
    s5d_d = din("s5d", [128, 4])
    s5h0_d = din("s5h0", [128, 2, 16, 16])
    maskB_d = din("maskB", [128, 4, 128])
    maskC_d = din("maskC", [128, 4, 128])
    gcw_d = din("gcw", [128, 12, 4])
    gsm_d = din("gsm", [128, 9])
    onorm_d = din("onorm", [128, 1])
    gst_d = din("gst", [NS_, 4, 128, 128])
    gcs_d = din("gcs", [128, 12, 3, 16])
    fcw_d = din("fcw", [128, 44, 3])
    fcs_d = din("fcs", [128, 44, 2, 16])
    consts_d = din("consts", [128, 6, 128])
    iota_d = din("iota", [128, SB])

    yT = dout("yT", [D, TT])
    s5o_d = dout("s5o", [128, 2, 16, 17])
    gdno_d = dout("gdno", [17, 4, 128, 128])
    gco_d = dout("gco", [128, 12, 17, 3])
    fco_d = dout("fco", [128, 44, 17, 2])

    with contextlib.ExitStack() as es:
        S = Sched(nc, es)

        def sb(name, shape, dt=F32):
            return es.enter_context(nc.sbuf_tensor("t_" + name, list(shape), dt))

        es2 = contextlib.ExitStack()

        def sb2(name, shape, dt=F32):
            return es2.enter_context(nc.sbuf_tensor("t_" + name, list(shape), dt))

        consts = sb("consts", [128, 6, 128])
        ident = consts[:, 0, :]
        ones = consts[:, 1, :]
        tri2 = consts[:, 2, :]
        triC = consts[:, 3, :]
        mstrict = consts[:, 4, :]
        onesR = sb("onesR", [128, 128], F32R)
        identR = sb("identR", [128, 128], F32R)
        gains = sb("gains", [128, 3, 8])
        s5ch = sb("s5ch", [128, 3, 16])
        s5d = sb("s5d", [128, 4])
        s5h0 = sb("s5h0", [128, 2, 16, 16])
        Bst = sb("Bst", [128, 16, 2, 128], F32R)
        Cst = sb("Cst", [128, 16, 2, 128], F32R)
        cosT = sb("cosT", [128, 16, SB])
        sinT = sb("sinT", [128, 16, SB])
        mag = sb("mag", [128, 16])
        carry = sb("carry", [128, 2, 16])
        cth = sb("cth", [128, 16])
        sth = sb("sth", [128, 16])
        cu = sb("cu", [128, 16])
        su = sb("su", [128, 16])
        s5o = sb("s5o", [128, 2, 16, 17])
        gcw = sb("gcw", [128, 12, 4])
        gsm = sb("gsm", [128, 9])
        nega = sb("nega", [128, 4])
        onorm = sb("onorm", [128, 1])
        gtail = sb("gtail", [128, 12, 3])
        gcs = sb("gcs", [128, 12, 3, 16])
        gco = sb("gco", [128, 12, 17, 3])
        fcw = sb("fcw", [128, 44, 3])
        ftail = sb("ftail", [128, 44, 2])
        fcs = sb("fcs", [128, 44, 2, 16])
        fco = sb("fco", [128, 44, 17, 2])
        Sst = [sb("Sst%d" % h, [128, 128]) for h in range(4)]
        wring = [sb("wr%d" % i, [128, 8, 128], F32R) for i in range(3)]
        wuring = [sb("wu%d" % i, [128, 8, 128], F32R) for i in range(4)]
        wdring = [sb("wd%d" % i, [128, 4, 128], F32R) for i in range(3)]
        NG = 2
        gt = [{k: sb("g%s%d" % (k, i), [128, 128]) for k in
               ("ktok", "ek", "kg", "vtok", "gmat", "bmat", "A", "dTi", "dTs", "Ebc", "M0", "M1", "N0", "N1",
                "P0", "P1", "attnT", "qgT", "Pb", "uc", "wcT", "vnew")} for i in range(NG)]
        gcol = sb("gcol", [128, 16, 4])
        s5bc = sb2("s5bc", [128, 3, 4, 64])
        s5bT = sb2("s5bT", [128, 2, 4, 64])
        s5cT = sb2("s5cT", [128, 2, 16, 16])
        maskB = sb2("maskB", [128, 4, 128])
        maskC = sb2("maskC", [128, 4, 128])
        iota = sb2("iota", [128, SB])
        psum = es.enter_context(nc.psum_tensor("psum", [128, 4096], F32))
        banks = [Res() for _ in range(8)]
        for i_, b_ in enumerate(banks):
            b_.idx = i_
        bfree = list(range(8))

        def bank(hold=False):
            i = bfree.pop(0)
            if not hold:
                bfree.append(i)
            return psum[:, i * 512:(i + 1) * 512], banks[i]

        def unhold(r):
            bfree.append(r.idx)

        free = []

        freeR = []
        rset = set()

        mins = {'f': 99, 'r': 99}

        def alloc():
            t = free.pop(0)
            mins['f'] = min(mins['f'], len(free))
            return t

        def allocR():
            t = freeR.pop(0)
            mins['r'] = min(mins['r'], len(freeR))
            return t

        def release(*ts):
            for t in ts:
                (freeR if id(t) in rset else free).append(t)

        def f(ap):
            return ap.bitcast(F32)

        def tt(e, out, a, b, op, R, W):
            S.op(e, lambda g: g.tensor_tensor(out=out, in0=a, in1=b, op=op), R, W)

        def stt(e, out, a, sc, b, op0, op1, R, W):
            S.op('dve', lambda g: g.scalar_tensor_tensor(out=out, in0=a, scalar=sc, in1=b, op0=op0, op1=op1), R, W)

        def ts(e, out, a, s1, s2, op0, op1, R, W):
            if s2 is None:
                S.op(e, lambda g: g.tensor_scalar(out=out, in0=a, scalar1=s1, scalar2=None, op0=op0), R, W)
            else:
                S.op(e, lambda g: g.tensor_scalar(out=out, in0=a, scalar1=s1, scalar2=s2, op0=op0, op1=op1), R, W)

        def act(out, in_, func, R, W, scale=1.0, bias=None):
            if bias is None:
                S.op('act', lambda g: g.activation(out=out, in_=in_, func=func, scale=scale), R, W)
            else:
                S.op('act', lambda g: g.activation(out=out, in_=in_, func=func, scale=scale, bias=bias), R, W)

        def cp(e, out, in_, R, W):
            if e == 'act':
                act(out, in_, AF.Copy, R, W)
            else:
                S.op(e, lambda g: g.tensor_copy(out=out, in_=in_), R, W)

        def mm(out, lhsT, rhs, R, W, start=True, stop=True):
            S.op('pe', lambda g: g.matmul(out, lhsT, rhs, start=start, stop=stop), R, W)

        def tr(out, in_, R, W):
            S.op('pe', lambda g: g.transpose(out, in_, ident), list(R) + [consts], W)

        for t, d in ((consts, consts_d), (gains, gains_d), (s5bc, s5bc_d), (s5bT, s5bT_d), (s5ch, s5ch_d),
                     (s5cT, s5cT_d), (s5d, s5d_d), (s5h0, s5h0_d), (maskB, maskB_d), (maskC, maskC_d),
                     (gcw, gcw_d), (gsm, gsm_d), (onorm, onorm_d), (gcs, gcs_d), (fcw, fcw_d), (fcs, fcs_d),
                     (iota, iota_d)):
            S.dma(t[:], d, writes=[t])
        cp('pool', onesR[:], ones, [consts], [onesR])
        cp('pool', identR[:], ident, [consts], [identR])
        S.op('dve', lambda g: g.memset(gtail[:], 0.0), writes=[gtail])
        S.op('dve', lambda g: g.memset(ftail[:], 0.0), writes=[ftail])
        S.op('dve', lambda g: g.memset(carry[:], 0.0), writes=[carry])
        S.op('pool', lambda g: g.memset(s5o[:], 0.0), writes=[s5o])
        S.op('pool', lambda g: g.memset(gco[:], 0.0), writes=[gco])
        S.op('pool', lambda g: g.memset(fco[:], 0.0), writes=[fco])
        for h in range(4):
            S.op('pool', lambda g, h=h: g.memset(Sst[h][:], 0.0), writes=[Sst[h]])
        act(nega[:], gsm[:, 0:4], AF.Exp, [gsm], [nega])
        ts('dve', nega[:], nega[:], -1.0, None, ALU.mult, None, [nega], [nega])

        TWO_PI = 2.0 * np.pi

        I32 = mybir.dt.int32
        INV2PI = 1.0 / TWO_PI

        def mk_scr(shape, nm):
            return (sb2(nm + "y", shape), sb2(nm + "ki", shape, I32), sb2(nm + "kf", shape))

        def sin_turns(dst, src, shift, scr, R):
            y, ki, kf = scr
            ts('dve', y[:], src, shift, INV2PI, ALU.add, ALU.mult, R, [y])
            cp('dve', ki[:], y[:], [y], [ki])
            cp('dve', kf[:], ki[:], [ki], [kf])
            tt('dve', y[:], y[:], kf[:], ALU.subtract, [y, kf], [y])
            ts('dve', kf[:], y[:], 0.5, None, ALU.is_gt, None, [y], [kf])
            tt('dve', y[:], y[:], kf[:], ALU.subtract, [y, kf], [y])
            ts('dve', kf[:], y[:], -0.5, None, ALU.is_lt, None, [y], [kf])
            tt('dve', y[:], y[:], kf[:], ALU.add, [y, kf], [y])
            act(dst, y[:], AF.Sin, [y], R, scale=TWO_PI)

        def s5_abar(are, aim, ldt, shape, nm):
            t = {k: sb2(nm + k, shape) for k in ("dt", "mg", "ang", "sn", "cs", "den", "p", "t1", "t2", "fr", "fi")}
            R = [s5bc, s5ch]
            act(t["dt"][:], ldt, AF.Exp, R, [t["dt"]])
            tt('dve', t["mg"][:], are, t["dt"][:], ALU.mult, R + [t["dt"]], [t["mg"]])
            act(t["mg"][:], t["mg"][:], AF.Exp, [t["mg"]], [t["mg"]])
            tt('dve', t["ang"][:], aim, t["dt"][:], ALU.mult, R + [t["dt"]], [t["ang"]])
            scr = mk_scr(shape, nm + "s")
            sin_turns(t["sn"][:], t["ang"][:], 0.0, scr, [t["ang"], t["sn"]])
            sin_turns(t["cs"][:], t["ang"][:], 0.5 * np.pi, scr, [t["ang"], t["cs"]])
            tt('dve', t["cs"][:], t["cs"][:], t["mg"][:], ALU.mult, [t["cs"], t["mg"]], [t["cs"]])
            tt('dve', t["sn"][:], t["sn"][:], t["mg"][:], ALU.mult, [t["sn"], t["mg"]], [t["sn"]])
            tt('dve', t["den"][:], are, are, ALU.mult, R, [t["den"]])
            tt('dve', t["t1"][:], aim, aim, ALU.mult, R, [t["t1"]])
            tt('dve', t["den"][:], t["den"][:], t["t1"][:], ALU.add, [t["den"], t["t1"]], [t["den"]])
            S.op('dve', lambda g: g.reciprocal(out=t["den"][:], in_=t["den"][:]), [t["den"]], [t["den"]])
            ts('dve', t["p"][:], t["cs"][:], -1.0, None, ALU.add, None, [t["cs"]], [t["p"]])
            tt('dve', t["t1"][:], t["p"][:], are, ALU.mult, R + [t["p"]], [t["t1"]])
            tt('dve', t["t2"][:], t["sn"][:], aim, ALU.mult, R + [t["sn"]], [t["t2"]])
            tt('dve', t["fr"][:], t["t1"][:], t["t2"][:], ALU.add, [t["t1"], t["t2"]], [t["fr"]])
            tt('dve', t["fr"][:], t["fr"][:], t["den"][:], ALU.mult, [t["fr"], t["den"]], [t["fr"]])
            tt('dve', t["t1"][:], t["sn"][:], are, ALU.mult, R + [t["sn"]], [t["t1"]])
            tt('dve', t["t2"][:], t["p"][:], aim, ALU.mult, R + [t["p"]], [t["t2"]])
            tt('dve', t["fi"][:], t["t1"][:], t["t2"][:], ALU.subtract, [t["t1"], t["t2"]], [t["fi"]])
            tt('dve', t["fi"][:], t["fi"][:], t["den"][:], ALU.mult, [t["fi"], t["den"]], [t["fi"]])
            return t

        tb_ = s5_abar(s5bc[:, 0], s5bc[:, 1], s5bc[:, 2], [128, 4, 64], "sB")
        Btil = sb2("Btil", [128, 2, 4, 64])
        tmpB = sb2("tmpB", [128, 4, 64])
        RB = [s5bT, tb_["fr"], tb_["fi"]]
        tt('dve', Btil[:, 0], tb_["fr"][:], s5bT[:, 0], ALU.mult, RB, [Btil])
        tt('dve', tmpB[:], tb_["fi"][:], s5bT[:, 1], ALU.mult, RB, [tmpB])
        tt('dve', Btil[:, 0], Btil[:, 0], tmpB[:], ALU.subtract, [Btil, tmpB], [Btil])
        tt('dve', Btil[:, 1], tb_["fr"][:], s5bT[:, 1], ALU.mult, RB + [Btil], [Btil])
        tt('dve', tmpB[:], tb_["fi"][:], s5bT[:, 0], ALU.mult, RB, [tmpB])
        tt('dve', Btil[:, 1], Btil[:, 1], tmpB[:], ALU.add, [Btil, tmpB], [Btil])
        for j in range(16):
            for c in range(2):
                for hf in range(2):
                    tt('dve', Bst[:, j, c, hf * 64:(hf + 1) * 64], Btil[:, c, j // 4, :],
                       maskB[:, j % 4, hf * 64:(hf + 1) * 64], ALU.mult, [Btil, maskB], [Bst])
        tc_ = s5_abar(s5ch[:, 0], s5ch[:, 1], s5ch[:, 2], [128, 16], "sC")
        cp('dve', cth[:], tc_["cs"][:], [tc_["cs"]], [cth])
        cp('dve', sth[:], tc_["sn"][:], [tc_["sn"]], [sth])
        cp('dve', mag[:], tc_["mg"][:], [tc_["mg"]], [mag])
        rmag = sb2("rmag", [128, 16])
        S.op('dve', lambda g: g.reciprocal(out=rmag[:], in_=mag[:]), [mag], [rmag])
        tt('dve', cu[:], cth[:], rmag[:], ALU.mult, [cth, rmag], [cu])
        tt('dve', su[:], sth[:], rmag[:], ALU.mult, [sth, rmag], [su])
        for j in range(16):
            for c in range(2):
                for gq in range(8):
                    if c == 0:
                        tt('pool', Cst[:, j, c, gq * 16:(gq + 1) * 16], s5cT[:, c, j, :],
                           maskC[:, j % 4, gq * 16:(gq + 1) * 16], ALU.mult, [s5cT, maskC], [Cst])
                    else:
                        stt('pool', Cst[:, j, c, gq * 16:(gq + 1) * 16], s5cT[:, c, j, :], -1.0,
                            maskC[:, j % 4, gq * 16:(gq + 1) * 16], ALU.mult, ALU.mult, [s5cT, maskC], [Cst])
        thr = sb2("thr", [128, 16])
        thk = sb2("thk", [128, 16], I32)
        thf = sb2("thf", [128, 16])
        ts('dve', thr[:], tc_["ang"][:], INV2PI, None, ALU.mult, None, [tc_["ang"]], [thr])
        cp('dve', thk[:], thr[:], [thr], [thk])
        cp('dve', thf[:], thk[:], [thk], [thf])
        tt('dve', thr[:], thr[:], thf[:], ALU.subtract, [thr, thf], [thr])
        ts('dve', thr[:], thr[:], TWO_PI, None, ALU.mult, None, [thr], [thr])
        tmpT = sb2("tmpT", [128, 16, SB])
        for j in range(16):
            ts('dve', tmpT[:, j, :], iota[:], thr[:, j:j + 1], None, ALU.mult, None, [iota, thr], [tmpT])
        scr = mk_scr([128, 4, SB], "tb")
        for q in range(4):
            qs = slice(q * 4, q * 4 + 4)
            sin_turns(sinT[:, qs, :], tmpT[:, qs, :], 0.0, scr, [tmpT, sinT])
            sin_turns(cosT[:, qs, :], tmpT[:, qs, :], 0.5 * np.pi, scr, [tmpT, cosT])

        for e in ('pe', 'act', 'dve', 'pool', 'sp'):
            for e2 in ('pe', 'act', 'dve', 'pool'):
                if e2 != e and S.cnt[e2]:
                    S._wait(e, (e2, S.cnt[e2]))
            for i in range(len(S.dsem)):
                if S.dcnt[i]:
                    S._wait(e, (i, S.dcnt[i]))
        es2.close()
        rem = nc.sbuf_bytes_remaining
        nslots = min(int((rem - 1024) // (SW * 4)), 40)
        print("sbuf remaining", rem, "nslots", nslots)
        slots = [sb("slot%d" % i, [128, SW], F32R) for i in range(nslots)]
        NR = 19
        freeR.extend(slots[:NR])
        rset.update(id(t) for t in slots[:NR])
        free.extend(slots[NR:])

        wstate = {'i': 0, 'u': 0, 'd': 0}

        def load_w(src_ap, kt_n, ncols):
            t = wring[wstate['i'] % 3]
            wstate['i'] += 1
            S.dma(t[:, 0:kt_n, :], src_ap, writes=[t])
            return t

        def rmsnorm(src, gidx, n, out_r, dim=1024, inplace=False):
            kt_n = len(src)
            pb, pr = bank()
            for kt in range(kt_n):
                sq = allocR()
                act(sq[:, 0:n], f(src[kt][:, 0:n]), AF.Square, [src[kt]], [sq])
                mm(pb[:, 0:n], onesR[:], sq[:, 0:n], [sq, onesR], [pr], start=(kt == 0), stop=(kt == kt_n - 1))
                release(sq)
            rstd = alloc()
            ts('dve', f(rstd[:, 0:n]), pb[:, 0:n], 1.0 / dim, EPS, ALU.mult, ALU.add, [pr], [rstd])
            act(f(rstd[:, 0:n]), f(rstd[:, 0:n]), AF.Ln, [rstd], [rstd])
            act(f(rstd[:, 0:n]), f(rstd[:, 0:n]), AF.Exp, [rstd], [rstd], scale=-0.5)
            outs = []
            for kt in range(kt_n):
                o = src[kt] if inplace else (allocR() if out_r else alloc())
                dst = o[:, 0:n] if out_r else f(o[:, 0:n])
                stt('dve' if kt % 2 == 0 else 'pool', dst, f(src[kt][:, 0:n]), gains[:, gidx, kt:kt + 1],
                    f(rstd[:, 0:n]), ALU.mult, ALU.mult, [src[kt], rstd, gains], [o])
                outs.append(o)
            release(rstd)
            return outs

        import os
        KSTOP = float(os.environ.get("KSTOP", "999"))

        class _Stop(Exception):
            pass

        def chk(stage):
            if os.environ.get("KVERB"):
                print("chk", stage, dict(S.cnt), S.dcnt)
            if stage >= KSTOP:
                raise _Stop()

        blocks = [(b * TB, TB, 'p') for b in range(SEQ // TB)] + [(SEQ, NS_, 's')]
        try:
          chk(0)
          for bi, (c0, n, mode) in enumerate(blocks):
              last_p = (mode == 'p' and c0 + n == SEQ)
              smp = (mode == 's')
              xs = []
              for kt in range(8):
                  s_ = alloc()
                  S.dma(f(s_[:, 0:n]), xT[kt * 128:(kt + 1) * 128, c0:c0 + n], writes=[s_])
                  xs.append(s_)
              xn = rmsnorm(xs, 0, n, True)
              release(*xs)
              chk(1 + 10 * bi)

              def inproj(ct, ncols=128):
                  wt = load_w(w_in[ct], 8, ncols)
                  pb, pr = bank()
                  for kt in range(8):
                      mm(pb[0:ncols, 0:n], wt[:, kt, 0:ncols], xn[kt][:, 0:n], [wt, xn[kt]], [pr],
                         start=(kt == 0), stop=(kt == 7))
                  return pb, pr, wt

              us = []
              for ct in range(4):
                  pb, pr, _ = inproj(ct)
                  u = allocR()
                  cp('act', u[:, 0:n], pb[:, 0:n], [pr], [u])
                  us.append(u)
              ygs = []
              for yt in range(4):
                  ypb, ypr = bank(hold=True)
                  ycnt = {'k': 0}
                  def s5_gen(jj, yt=yt, ypb=ypb, ypr=ypr, ycnt=ycnt):
                      j = yt * 4 + jj
                      zr_b, zr_r = bank()
                      zi_b, zi_r = bank()
                      mm(zr_b[:, 0:n], Bst[:, j, 0, :], us[yt][:, 0:n], [Bst, us[yt]], [zr_r])
                      yield
                      mm(zi_b[:, 0:n], Bst[:, j, 1, :], us[yt][:, 0:n], [Bst, us[yt]], [zi_r])
                      yield
                      hr = allocR()
                      hi = allocR()
                      if smp:
                          t1 = alloc()
                          t2 = alloc()
                          stt('dve', f(t1[:, 0:n]), s5h0[:, 0, j, :], cth[:, j:j + 1], zr_b[:, 0:n], ALU.mult, ALU.add,
                              [s5h0, cth, zr_r], [t1])
                          yield
                          ts('dve', f(t2[:, 0:n]), s5h0[:, 1, j, :], sth[:, j:j + 1], None, ALU.mult, None, [s5h0, sth], [t2])
                          yield
                          tt('dve', hr[:, 0:n], f(t1[:, 0:n]), f(t2[:, 0:n]), ALU.subtract, [t1, t2], [hr])
                          yield
                          stt('dve', f(t1[:, 0:n]), s5h0[:, 1, j, :], cth[:, j:j + 1], zi_b[:, 0:n], ALU.mult, ALU.add,
                              [s5h0, cth, zi_r], [t1])
                          yield
                          stt('dve', hi[:, 0:n], s5h0[:, 0, j, :], sth[:, j:j + 1], f(t1[:, 0:n]), ALU.mult, ALU.add,
                              [s5h0, sth, t1], [hi])
                          yield
                          release(t1, t2)
                          cp('pool', s5o[:, 0, j, 1:17], f(hr[:, 0:n]), [hr], [s5o])
                          yield
                          cp('pool', s5o[:, 1, j, 1:17], f(hi[:, 0:n]), [hi], [s5o])
                          yield
                      else:
                          Zr = alloc()
                          Zi = alloc()
                          cp('act', f(Zr[:, 0:n]), zr_b[:, 0:n], [zr_r], [Zr])
                          yield
                          cp('act', f(Zi[:, 0:n]), zi_b[:, 0:n], [zi_r], [Zi])
                          yield
                          t1 = alloc()
                          t2 = alloc()
                          for sbk in range(n // SB):
                              cs = slice(sbk * SB, (sbk + 1) * SB)
                              cT = cosT[:, j, :]
                              sT = sinT[:, j, :]
                              RT = [cosT, sinT]
                              tt('dve', f(t1[:, cs]), f(Zr[:, cs]), cT, ALU.mult, RT + [Zr], [t1])
                              yield
                              tt('pool', f(t2[:, cs]), f(Zi[:, cs]), sT, ALU.mult, RT + [Zi], [t2])
                              yield
                              tt('dve', f(t1[:, cs]), f(t1[:, cs]), f(t2[:, cs]), ALU.add, [t1, t2], [t1])
                              yield
                              tt('pool', f(t2[:, cs]), f(Zi[:, cs]), cT, ALU.mult, RT + [Zi, t2], [t2])
                              yield
                              tt('dve', f(Zi[:, cs]), f(Zr[:, cs]), sT, ALU.mult, RT + [Zr], [Zi])
                              yield
                              tt('pool', f(t2[:, cs]), f(t2[:, cs]), f(Zi[:, cs]), ALU.subtract, [t2, Zi], [t2])
                              yield
                              S.op('dve', lambda g, cs=cs: g.tensor_tensor_scan(
                                  out=f(Zr[:, cs]), data0=mag[:, j:j + 1].to_broadcast([128, SB]), data1=f(t1[:, cs]),
                                  initial=carry[:, 0, j:j + 1], op0=ALU.mult, op1=ALU.add), [mag, t1, carry], [Zr])
                              yield
                              S.op('dve', lambda g, cs=cs: g.tensor_tensor_scan(
                                  out=f(Zi[:, cs]), data0=mag[:, j:j + 1].to_broadcast([128, SB]), data1=f(t2[:, cs]),
                                  initial=carry[:, 1, j:j + 1], op0=ALU.mult, op1=ALU.add), [mag, t2, carry], [Zi])
                              yield
                              tt('dve', f(t1[:, cs]), f(Zr[:, cs]), cT, ALU.mult, RT + [Zr], [t1])
                              yield
                              tt('pool', f(t2[:, cs]), f(Zi[:, cs]), sT, ALU.mult, RT + [Zi], [t2])
                              yield
                              tt('dve', hr[:, cs], f(t1[:, cs]), f(t2[:, cs]), ALU.subtract, [t1, t2], [hr])
                              yield
                              tt('pool', f(t2[:, cs]), f(Zi[:, cs]), cT, ALU.mult, RT + [Zi, t2], [t2])
                              yield
                              tt('dve', f(t1[:, cs]), f(Zr[:, cs]), sT, ALU.mult, RT + [Zr, t1], [t1])
                              yield
                              tt('pool', hi[:, cs], f(t2[:, cs]), f(t1[:, cs]), ALU.add, [t1, t2], [hi])
                              yield
                              e_ = (sbk + 1) * SB - 1
                              he_r = f(hr[:, e_:e_ + 1])
                              he_i = f(hi[:, e_:e_ + 1])
                              ts('dve', f(t1[:, 0:1]), he_i, su[:, j:j + 1], None, ALU.mult, None, [hi, su], [t1])
                              yield
                              stt('dve', carry[:, 0, j:j + 1], he_r, cu[:, j:j + 1], f(t1[:, 0:1]), ALU.mult, ALU.subtract,
                                  [hr, cu, t1], [carry])
                              yield
                              ts('dve', f(t1[:, 0:1]), he_r, su[:, j:j + 1], None, ALU.mult, None, [hr, su], [t1])
                              yield
                              stt('dve', carry[:, 1, j:j + 1], he_i, cu[:, j:j + 1], f(t1[:, 0:1]), ALU.mult, ALU.add,
                                  [hi, cu, t1], [carry])
                              yield
                          if last_p:
                              cp('pool', s5o[:, 0, j, 0:1], f(hr[:, n - 1:n]), [hr], [s5o])
                              yield
                              cp('pool', s5o[:, 1, j, 0:1], f(hi[:, n - 1:n]), [hi], [s5o])
                              yield
                          release(Zr, Zi, t1, t2)
                      mm(ypb[:, 0:n], Cst[:, j, 0, :], hr[:, 0:n], [Cst, hr], [ypr], start=(ycnt['k'] == 0), stop=False)
                      ycnt['k'] += 1
                      mm(ypb[:, 0:n], Cst[:, j, 1, :], hi[:, 0:n], [Cst, hi], [ypr], start=False, stop=(ycnt['k'] == 7))
                      ycnt['k'] += 1
                      release(hr, hi)
                  for jp in (0, 2):
                      gens = [s5_gen(jp), s5_gen(jp + 1)]
                      while gens:
                          for g_ in list(gens):
                              try:
                                  next(g_)
                              except StopIteration:
                                  gens.remove(g_)
                  y = alloc()
                  t1 = alloc()
                  stt('dve', f(y[:, 0:n]), f(us[yt][:, 0:n]), s5d[:, yt:yt + 1], ypb[:, 0:n], ALU.mult, ALU.add,
                      [us[yt], s5d, ypr], [y])
                  tt('pool', f(t1[:, 0:n]), f(y[:, 0:n]), f(y[:, 0:n]), ALU.mult, [y], [t1])
                  ts('dve', f(t1[:, 0:n]), f(t1[:, 0:n]), 0.044715, 1.0, ALU.mult, ALU.add, [t1], [t1])
                  tt('pool', f(t1[:, 0:n]), f(t1[:, 0:n]), f(y[:, 0:n]), ALU.mult, [t1, y], [t1])
                  act(f(t1[:, 0:n]), f(t1[:, 0:n]), AF.Tanh, [t1], [t1], scale=GELU_C)
                  stt('dve', f(t1[:, 0:n]), f(t1[:, 0:n]), 1.0, f(y[:, 0:n]), ALU.add, ALU.mult, [t1, y], [t1])
                  yg = allocR()
                  ts('dve', yg[:, 0:n], f(t1[:, 0:n]), 0.5, None, ALU.mult, None, [t1], [yg])
                  release(y, t1)
                  ygs.append(yg)
                  unhold(ypr)
              release(*us)
              chk(2 + 10 * bi)
              ymix = []
              for m in range(4):
                  wa = load_w(w_glu[m], 4, 128)
                  wb = load_w(w_glu[m + 4], 4, 128)
                  pa, par = bank()
                  pb, pbr = bank()
                  for kt in range(4):
                      mm(pa[:, 0:n], wa[:, kt, :], ygs[kt][:, 0:n], [wa, ygs[kt]], [par], start=(kt == 0), stop=(kt == 3))
                  for kt in range(4):
                      mm(pb[:, 0:n], wb[:, kt, :], ygs[kt][:, 0:n], [wb, ygs[kt]], [pbr], start=(kt == 0), stop=(kt == 3))
                  t1 = alloc()
                  act(f(t1[:, 0:n]), pb[:, 0:n], AF.Tanh, [pbr], [t1], scale=0.5)
                  ts('dve', f(t1[:, 0:n]), f(t1[:, 0:n]), 0.5, 0.5, ALU.mult, ALU.add, [t1], [t1])
                  o = allocR()
                  tt('dve', o[:, 0:n], f(t1[:, 0:n]), pa[:, 0:n], ALU.mult, [t1, par], [o])
                  release(t1)
                  ymix.append(o)
              release(*ygs)
              chk(3 + 10 * bi)

              pb, pr, wba = inproj(20, 8)
              ngrp = (n + 127) // 128
              bac_b, bac_r = bank(hold=True)
              for gi in range(ngrp):
                  gn = min(128, n - gi * 128)
                  for kt in range(8):
                      mm(bac_b[0:gn, gi * 8:gi * 8 + 8], xn[kt][:, gi * 128:gi * 128 + gn], wba[:, kt, 0:8],
                         [xn[kt], wba], [bac_r], start=(kt == 0), stop=(kt == 7))
              for gi in range(ngrp):
                  gn = min(128, n - gi * 128)
                  P_ = slice(0, gn)
                  R0 = [bac_r]
                  W0 = [gcol]
                  act(gcol[P_, gi, :], bac_b[P_, gi * 8:gi * 8 + 4], AF.Tanh, R0, W0, scale=0.5)
                  ts('dve', gcol[P_, 8 + gi, :], gcol[P_, gi, :], -0.5, -0.5, ALU.mult, ALU.add, [gcol], W0)
                  ts('dve', gcol[P_, gi, :], gcol[P_, gi, :], 0.5, 0.5, ALU.mult, ALU.add, [gcol], W0)
                  tt('dve', gcol[P_, 15, :], bac_b[P_, gi * 8 + 4:gi * 8 + 8], gsm[P_, 4:8], ALU.add, R0 + [gsm], W0)
                  act(gcol[P_, 12, :], gcol[P_, 15, :], AF.Abs, [gcol], W0)
                  act(gcol[P_, 12, :], gcol[P_, 12, :], AF.Exp, [gcol], W0, scale=-1.0)
                  act(gcol[P_, 12, :], gcol[P_, 12, :], AF.Ln, [gcol], W0, bias=1.0)
                  stt('dve', gcol[P_, 15, :], gcol[P_, 15, :], 0.0, gcol[P_, 12, :], ALU.max, ALU.add, [gcol], W0)
                  tt('dve', gcol[P_, 4 + gi, :], gcol[P_, 15, :], nega[P_, :], ALU.mult, [gcol, nega], W0)
              unhold(bac_r)
              chk(3.1 + 10 * bi)
              zts = []
              oTs = []
              def head_front(h):
                  qkv = []
                  for which in range(3):
                      ci = which * 4 + h
                      pb, pr, _ = inproj(4 + ci)
                      raw = alloc()
                      cp('act', f(raw[:, 4:4 + n]), pb[:, 0:n], [pr], [raw])
                      acc = alloc()
                      if smp:
                          ts('dve', f(acc[:, 0:n]), gcs[:, ci, 0, :], gcw[:, ci, 0:1], None, ALU.mult, None, [gcs, gcw], [acc])
                          for jx in (1, 2):
                              stt('dve', f(acc[:, 0:n]), gcs[:, ci, jx, :], gcw[:, ci, jx:jx + 1], f(acc[:, 0:n]),
                                  ALU.mult, ALU.add, [gcs, gcw, acc], [acc])
                          stt('dve', f(acc[:, 0:n]), f(raw[:, 4:4 + n]), gcw[:, ci, 3:4], f(acc[:, 0:n]),
                              ALU.mult, ALU.add, [raw, gcw, acc], [acc])
                          cp('pool', gco[:, ci, 1:17, 0], gcs[:, ci, 1, :], [gcs], [gco])
                          cp('pool', gco[:, ci, 1:17, 1], gcs[:, ci, 2, :], [gcs], [gco])
                          cp('pool', gco[:, ci, 1:17, 2], f(raw[:, 4:4 + n]), [raw], [gco])
                      else:
                          cp('pool', f(raw[:, 1:4]), gtail[:, ci, :], [gtail], [raw])
                          ts('dve', f(acc[:, 0:n]), f(raw[:, 1:1 + n]), gcw[:, ci, 0:1], None, ALU.mult, None, [raw, gcw], [acc])
                          for jx in (1, 2, 3):
                              stt('dve' if jx != 2 else 'pool', f(acc[:, 0:n]), f(raw[:, 1 + jx:1 + jx + n]),
                                  gcw[:, ci, jx:jx + 1], f(acc[:, 0:n]), ALU.mult, ALU.add, [raw, gcw, acc], [acc])
                          cp('pool', gtail[:, ci, :], f(raw[:, 1 + n:4 + n]), [raw], [gtail])
                          if last_p:
                              cp('pool', gco[:, ci, 0, :], f(raw[:, 1 + n:4 + n]), [raw], [gco])
                      act(f(raw[:, 0:n]), f(acc[:, 0:n]), AF.Tanh, [acc], [raw], scale=0.5)
                      ts('dve', f(raw[:, 0:n]), f(raw[:, 0:n]), 0.5, 0.5, ALU.mult, ALU.add, [raw], [raw])
                      tt('pool', f(acc[:, 0:n]), f(acc[:, 0:n]), f(raw[:, 0:n]), ALU.mult, [acc, raw], [acc])
                      if which < 2:
                          sqr = allocR()
                          act(sqr[:, 0:n], f(acc[:, 0:n]), AF.Square, [acc], [sqr])
                          nb, nr = bank()
                          mm(nb[:, 0:n], onesR[:], sqr[:, 0:n], [onesR, sqr], [nr])
                          release(sqr)
                          ts('dve', f(raw[:, 0:n]), nb[:, 0:n], EPS, None, ALU.add, None, [nr], [raw])
                          act(f(raw[:, 0:n]), f(raw[:, 0:n]), AF.Ln, [raw], [raw])
                          act(f(raw[:, 0:n]), f(raw[:, 0:n]), AF.Exp, [raw], [raw], scale=-0.5)
                          stt('dve', f(acc[:, 0:n]), f(acc[:, 0:n]), (128.0 ** -0.5) if which == 0 else 1.0,
                              f(raw[:, 0:n]), ALU.mult, ALU.mult, [acc, raw], [acc])
                      release(raw)
                      qkv.append(acc)
                  qT, kT, vT = qkv
                  chk(3.2 + 10 * bi)
                  pb, pr, _ = inproj(16 + h)
                  zt = alloc()
                  cp('act', f(zt[:, 0:n]), pb[:, 0:n], [pr], [zt])
                  chk(3.21 + 10 * bi)
                  ob, orr = bank(hold=True)
                  return dict(qT=qT, kT=kT, vT=vT, zt=zt, ob=ob, orr=orr)

              def head_sample(h, hd):
                  qT, kT, vT, ob, orr = hd['qT'], hd['kT'], hd['vT'], hd['ob'], hd['orr']
                  G = gt[0]
                  act(gcol[0:n, 13, :], gcol[0:n, 4, :], AF.Exp, [gcol], [gcol])
                  ts('dve', G["bmat"][0:n, :], ones[0:n, :], gcol[0:n, 8, h:h + 1], None, ALU.mult, None, [consts, gcol], [G["bmat"]])
                  ts('dve', G["gmat"][0:n, :], ones[0:n, :], gcol[0:n, 13, h:h + 1], None, ALU.mult, None, [consts, gcol], [G["gmat"]])
                  bb, bbr = bank()
                  mm(bb[:, 0:n], G["bmat"][0:n, :], ident[0:n, 0:n], [G["bmat"], consts], [bbr])
                  mm(bb[:, 64:64 + n], G["gmat"][0:n, :], ident[0:n, 0:n], [G["gmat"], consts], [bbr])
                  cp('act', G["Ebc"][:, 0:128], bb[:, 0:128], [bbr], [G["Ebc"]])
                  variants = []
                  for gg in gt:
                      variants.append(dict(S0=gg["M0"], vn=gg["vnew"], M1=gg["M1"], N0=gg["N0"], N1=gg["N1"]))
                      variants.append(dict(S0=gg["P0"], vn=gg["uc"], M1=gg["P1"], N0=gg["attnT"], N1=gg["qgT"]))
                      variants.append(dict(S0=gg["ek"], vn=gg["wcT"], M1=gg["kg"], N0=gg["vtok"], N1=gg["Pb"]))

                  def sample_gen(s, T):
                      S0 = T["S0"]
                      S.dma(S0[:], gst_d[s, h], writes=[S0])
                      yield
                      kb, kbr = bank()
                      mm(kb[:, 0:1], S0[:], f(kT[:, s:s + 1]), [S0, kT], [kbr])
                      yield
                      stt('dve', T["vn"][:, 0:1], kb[:, 0:1], G["Ebc"][:, 64 + s:65 + s], f(vT[:, s:s + 1]),
                          ALU.mult, ALU.subtract, [kbr, G["Ebc"], vT], [T["vn"]])
                      yield
                      ts('dve', T["vn"][:, 0:1], T["vn"][:, 0:1], G["Ebc"][:, s:s + 1], None, ALU.mult, None,
                         [T["vn"], G["Ebc"]], [T["vn"]])
                      yield
                      S.op('act', lambda g: g.activation(out=T["M1"][:], in_=ones, func=AF.Copy,
                           scale=T["vn"][:, 0:1]), [consts, T["vn"]], [T["M1"]])
                      yield
                      mm(kb[:, 128:256], T["M1"][:], ident, [T["M1"], consts], [kbr])
                      yield
                      ts('dve', T["N0"][:], kb[:, 128:256], f(kT[:, s:s + 1]), None, ALU.mult, None, [kbr, kT], [T["N0"]])
                      yield
                      stt('dve', T["N1"][:], S0[:], G["Ebc"][:, 64 + s:65 + s], T["N0"][:], ALU.mult, ALU.add,
                          [S0, G["Ebc"], T["N0"]], [T["N1"]])
                      yield
                      S.dma(gdno_d[1 + s, h], T["N1"][:], reads=[T["N1"]])
                      mm(ob[:, s:s + 1], T["N1"][:], f(qT[:, s:s + 1]), [T["N1"], qT], [orr])

                  nv = 6
                  for s0 in range(0, n, nv):
                      gens = [sample_gen(s0 + i, variants[i]) for i in range(min(nv, n - s0))]
                      while gens:
                          for g_ in list(gens):
                              try:
                                  next(g_)
                              except StopIteration:
                                  gens.remove(g_)

              def group_gen(h, gi, G, hd):
                  qT, kT, vT, ob, orr = hd['qT'], hd['kT'], hd['vT'], hd['ob'], hd['orr']
                  gs = slice(gi * 128, (gi + 1) * 128)
                  cb, cbr = bank()
                  mm(cb[:, 0:1], tri2, gcol[:, 4 + gi, h:h + 1], [consts, gcol], [cbr])
                  mm(cb[:, 1:2], triC, gcol[:, 4 + gi, h:h + 1], [consts, gcol], [cbr])
                  gck = Res()
                  cp('dve', G["vnew"][:, 0:1], cb[:, 0:1], [cbr], [G["vnew"]])
                  act(G["vnew"][:, 1:3], cb[:, 0:2], AF.Exp, [cbr], [G["vnew"]])
                  yield
                  chk(3.22 + 10 * bi)
                  tb1, tbr = bank(hold=True)
                  tr(tb1[:, 0:128], f(kT[:, gs]), [kT], [tbr])
                  tr(tb1[:, 128:256], f(vT[:, gs]), [vT], [tbr])
                  yield
                  chk(3.225 + 10 * bi)
                  ts('dve', G["ek"][:], tb1[:, 0:128], G["vnew"][:, 1:2], None, ALU.mult, None, [tbr, G["vnew"]], [G["ek"]])
                  chk(3.226 + 10 * bi)
                  ts('dve', G["kg"][:], tb1[:, 0:128], G["vnew"][:, 2:3], None, ALU.mult, None, [tbr, G["vnew"]], [G["kg"]])
                  chk(3.227 + 10 * bi)
                  cp('dve', G["vtok"][:], tb1[:, 128:256], [tbr], [G["vtok"]])
                  yield
                  chk(3.23 + 10 * bi)
                  S.op('act', lambda g, G=G, gi=gi, h=h: g.activation(out=G["gmat"][:], in_=ones, func=AF.Copy,
                       scale=gcol[:, 4 + gi, h:h + 1]), [consts, gcol], [G["gmat"]])
                  S.op('act', lambda g, G=G, gi=gi, h=h: g.activation(out=G["bmat"][:], in_=ones, func=AF.Copy,
                       scale=gcol[:, gi, h:h + 1]), [consts, gcol], [G["bmat"]])
                  mm(tb1[:, 256:384], G["gmat"][:], tri2, [G["gmat"], consts], [tbr])
                  mm(tb1[:, 384:512], G["bmat"][:], ident, [G["bmat"], consts], [tbr])
                  yield
                  chk(3.24 + 10 * bi)
                  GCB = tb1[:, 256:384]
                  Bbc = tb1[:, 384:512]
                  ts('dve', G["A"][:], GCB, G["vnew"][:, 0:1], 0.0, ALU.subtract, ALU.min, [tbr, G["vnew"]], [G["A"]])
                  act(G["A"][:], G["A"][:], AF.Exp, [G["A"]], [G["A"]])
                  tt('pool', G["dTi"][:], G["A"][:], tri2, ALU.mult, [G["A"], consts], [G["dTi"]])
                  tt('pool', G["dTs"][:], G["A"][:], mstrict, ALU.mult, [G["A"], consts], [G["dTs"]])
                  act(G["Ebc"][:], GCB, AF.Exp, [tbr], [G["Ebc"]])
                  yield
                  chk(3.3 + 10 * bi)
                  kkb, kkr = bank()
                  mm(kkb[:, 0:128], f(kT[:, gs]), f(kT[:, gs]), [kT], [kkr])
                  mm(kkb[:, 128:256], f(kT[:, gs]), f(qT[:, gs]), [kT, qT], [kkr])
                  yield
                  stt('dve', G["M0"][:], kkb[:, 0:128], gcol[:, 8 + gi, h:h + 1], G["dTs"][:], ALU.mult, ALU.mult,
                      [kkr, gcol, G["dTs"]], [G["M0"]])
                  tt('dve', G["attnT"][:], kkb[:, 128:256], G["dTi"][:], ALU.mult, [kkr, G["dTi"]], [G["attnT"]])
                  tt('pool', G["qgT"][:], f(qT[:, gs]), G["Ebc"][:], ALU.mult, [qT, G["Ebc"]], [G["qgT"]])
                  yield
                  tr(kkb[:, 256:384], G["M0"][:], [G["M0"]], [kkr])
                  yield
                  cp('dve', G["N0"][:], kkb[:, 256:384], [kkr], [G["N0"]])
                  tt('pool', G["P0"][:], G["M0"][:], ident, ALU.add, [G["M0"], consts], [G["P0"]])
                  yield
                  chk(3.4 + 10 * bi)
                  Mc, Nc, Pc = "M0", "N0", "P0"
                  for lvl in range(5):
                      Mn, Nn, Pn = ("M1", "N1", "P1") if lvl % 2 == 0 else ("M0", "N0", "P0")
                      lb, lbr = bank()
                      mm(lb[:, 0:128], G[Mc][:], G[Nc][:], [G[Mc], G[Nc]], [lbr])
                      if lvl < 4:
                          mm(lb[:, 128:256], G[Nc][:], G[Mc][:], [G[Mc], G[Nc]], [lbr])
                      cp('act', G[Nn][:], lb[:, 0:128], [lbr], [G[Nn]])
                      yield
                      if lvl < 4:
                          cp('dve', G[Mn][:], lb[:, 128:256], [lbr], [G[Mn]])
                      mm(lb[:, 256:384], G[Nn][:], G[Pc][:], [G[Nn], G[Pc]], [lbr])
                      yield
                      tt('dve', G[Pn][:], lb[:, 256:384], G[Pc][:], ALU.add, [lbr, G[Pc]], [G[Pn]])
                      yield
                      Mc, Nc, Pc = Mn, Nn, Pn
                  tt('dve', G["Pb"][:], Bbc, G[Pc][:], ALU.mult, [tbr, G[Pc]], [G["Pb"]])
                  yield
                  chk(3.5 + 10 * bi)
                  unhold(tbr)
                  ub, ubr = bank()
                  mm(ub[:, 0:128], G["Pb"][:], G["vtok"][:], [G["Pb"], G["vtok"]], [ubr])
                  mm(ub[:, 128:256], G["ek"][:], G["Pb"][:], [G["Pb"], G["ek"]], [ubr])
                  yield
                  cp('act', G["uc"][:], ub[:, 0:128], [ubr], [G["uc"]])
                  cp('dve', G["wcT"][:], ub[:, 128:256], [ubr], [G["wcT"]])
                  yield
                  chk(3.6 + 10 * bi)
                  for hf in range(2):
                      r = slice(hf * 64, hf * 64 + 64)
                      cg = slice(gi * 128 + hf * 64, gi * 128 + hf * 64 + 64)
                      wb_, wbr = bank()
                      mm(wb_[:, 0:128], G["wcT"][:], Sst[h][:], [G["wcT"], Sst[h]], [wbr])
                      yield
                      tt('dve', G["vnew"][r, 0:128] if False else G["P0"][r, :], G["uc"][r, :], wb_[r, 0:128], ALU.subtract,
                         [G["uc"], wbr], [G["P0"]])
                      vn = G["P0"]
                      mm(ob[:, cg], Sst[h][:], G["qgT"][:, r], [Sst[h], G["qgT"]], [orr], start=True, stop=False)
                      mm(ob[:, cg], vn[r, :], G["attnT"][r, r], [vn, G["attnT"]], [orr], start=False, stop=True)
                      mm(wb_[:, 128:256], G["kg"][r, :], vn[r, :], [G["kg"], vn], [wbr])
                      yield
                      ec = hf * 64 + 63
                      stt('dve', Sst[h][:], Sst[h][:], G["Ebc"][:, ec:ec + 1], wb_[:, 128:256], ALU.mult, ALU.add,
                          [Sst[h], G["Ebc"], wbr], [Sst[h]])

              def head_back(h, hd):
                  qT, kT, vT, zt, ob, orr = hd['qT'], hd['kT'], hd['vT'], hd['zt'], hd['ob'], hd['orr']
                  if last_p:
                      S.dma(gdno_d[0, h], Sst[h][:], reads=[Sst[h]])
                  chk(3.7 + 10 * bi)
                  release(qT, kT, vT)
                  oT = alloc()
                  cp('act', f(oT[:, 0:n]), ob[:, 0:n], [orr], [oT])
                  t1 = alloc()
                  sqr = allocR()
                  act(sqr[:, 0:n], f(oT[:, 0:n]), AF.Square, [oT], [sqr])
                  nb, nr = bank()
                  mm(nb[:, 0:n], onesR[:], sqr[:, 0:n], [onesR, sqr], [nr])
                  release(sqr)
                  ts('dve', f(t1[:, 0:n]), nb[:, 0:n], 1.0 / 128, EPS, ALU.mult, ALU.add, [nr], [t1])
                  act(f(t1[:, 0:n]), f(t1[:, 0:n]), AF.Ln, [t1], [t1])
                  act(f(t1[:, 0:n]), f(t1[:, 0:n]), AF.Exp, [t1], [t1], scale=-0.5)
                  stt('dve', f(oT[:, 0:n]), f(oT[:, 0:n]), onorm[:, 0:1], f(t1[:, 0:n]), ALU.mult, ALU.mult, [oT, onorm, t1], [oT])
                  act(f(t1[:, 0:n]), f(zt[:, 0:n]), AF.Tanh, [zt], [t1], scale=0.5)
                  stt('pool', f(t1[:, 0:n]), f(t1[:, 0:n]), 1.0, f(zt[:, 0:n]), ALU.add, ALU.mult, [t1, zt], [t1])
                  yo = allocR()
                  stt('dve', yo[:, 0:n], f(oT[:, 0:n]), 0.5, f(t1[:, 0:n]), ALU.mult, ALU.mult, [oT, t1], [yo])
                  release(oT, t1, zt)
                  unhold(orr)
                  ymix.append(yo)

              for hp in (0, 2):
                  hds = {h: head_front(h) for h in (hp, hp + 1)}
                  if smp:
                      for h in (hp, hp + 1):
                          head_sample(h, hds[h])
                  else:
                      for gi in range(ngrp):
                          gens = [group_gen(h, gi, gt[h % 2], hds[h]) for h in (hp, hp + 1)]
                          while gens:
                              for g_ in list(gens):
                                  try:
                                      next(g_)
                                  except StopIteration:
                                      gens.remove(g_)
                  for h in (hp, hp + 1):
                      head_back(h, hds[h])
              release(*xn)
              chk(4 + 10 * bi)

              x1 = []
              for m in range(8):
                  wt = load_w(w_out[m], 8, 128)
                  pb, pr = bank()
                  for kt in range(8):
                      mm(pb[:, 0:n], wt[:, kt, :], ymix[kt][:, 0:n], [wt, ymix[kt]], [pr], start=(kt == 0), stop=(kt == 7))
                  xr = alloc()
                  S.dma(f(xr[:, 0:n]), xT[m * 128:(m + 1) * 128, c0:c0 + n], writes=[xr])
                  tt('dve', f(xr[:, 0:n]), f(xr[:, 0:n]), pb[:, 0:n], ALU.add, [xr, pr], [xr])
                  x1.append(xr)
              release(*ymix)
              chk(5 + 10 * bi)
              xn2 = rmsnorm(x1, 1, n, True)
              for (h0, h1) in HGROUPS:
                  acts = []
                  acts_d = {}

                  def ffn_gen(j):
                      wus = [wuring[(2 * wstate['u']) % 4], wuring[(2 * wstate['u'] + 1) % 4]]
                      wstate['u'] += 1
                      for half in range(2):
                          S.dma(wus[half][:], w_up[half * NHT + j], writes=[wus[half]])
                      yield
                      cvd = []
                      pbs = []
                      for half in range(2):
                          pb, pr = bank(hold=True)
                          for kt in range(8):
                              mm(pb[:, 0:n], wus[half][:, kt, :], xn2[kt][:, 0:n], [wus[half], xn2[kt]], [pr],
                                 start=(kt == 0), stop=(kt == 7))
                          pbs.append((pb, pr))
                      yield
                      rss = []
                      for half in range(2):
                          ci = half * NHT + j
                          pb, pr = pbs[half]
                          if smp:
                              if half == 0:
                                  sm = alloc()
                              raw_ap = f(sm[:, 4:4 + n])
                              acc_ap = f(sm[:, 32 + 32 * half:32 + 32 * half + n])
                              cp('act', raw_ap, pb[:, 0:n], [pr], [sm])
                              e1 = 'dve'
                              ts(e1, acc_ap, fcs[:, ci, 0, :], fcw[:, ci, 0:1], None, ALU.mult, None, [fcs, fcw], [sm])
                              stt(e1, acc_ap, fcs[:, ci, 1, :], fcw[:, ci, 1:2], acc_ap, ALU.mult, ALU.add,
                                  [fcs, fcw, sm], [sm])
                              stt(e1, acc_ap, raw_ap, fcw[:, ci, 2:3], acc_ap, ALU.mult, ALU.add,
                                  [fcw, sm], [sm])
                              cp('pool', fco[:, ci, 1:17, 0], fcs[:, ci, 1, :], [fcs], [fco])
                              cp('pool', fco[:, ci, 1:17, 1], raw_ap, [sm], [fco])
                              cvd.append((acc_ap, sm, sm if half == 1 else None))
                              unhold(pr)
                          else:
                              rs = [allocR() for _ in range(3)]
                              for k in range(3):
                                  if k < 2:
                                      S.op('act', lambda g, k=k, rs=rs, pb=pb, ci=ci: g.activation(
                                          out=rs[k][:, 4:4 + n], in_=pb[:, 0:n], func=AF.Copy, scale=fcw[:, ci, k:k + 1]),
                                          [pr, fcw], [rs[k]])
                                  else:
                                      ts('dve', rs[k][:, 4:4 + n], pb[:, 0:n], fcw[:, ci, k:k + 1], None, ALU.mult, None, [pr, fcw], [rs[k]])
                              ts('dve', rs[0][:, 2:4], ftail[:, ci, 0:2], fcw[:, ci, 0:1], None, ALU.mult, None, [ftail, fcw], [rs[0]])
                              ts('dve', rs[1][:, 3:4], ftail[:, ci, 1:2], fcw[:, ci, 1:2], None, ALU.mult, None, [ftail, fcw], [rs[1]])
                              cp('dve', ftail[:, ci, :], pb[:, n - 2:n], [pr], [ftail])
                              if last_p:
                                  cp('pool', fco[:, ci, 0, :], ftail[:, ci, :], [ftail], [fco])
                              unhold(pr)
                              rss.append(rs)
                      if not smp:
                          for half in range(2):
                              rs = rss[half]
                              cb, cr = bank(hold=True)
                              for k in range(3):
                                  mm(cb[:, 0:n], identR[:], rs[k][:, 2 + k:2 + k + n], [identR, rs[k]], [cr], start=(k == 0), stop=(k == 2))
                              release(*rs)
                              cvd.append((cb[:, 0:n], cr, None))
                      (g_ap, g_res, g_sl), (u_ap, u_res, u_sl) = cvd
                      if smp:
                          t1, t1_ap = sm, f(sm[:, 96:96 + n])
                      else:
                          t1 = allocR()
                          t1_ap = t1[:, 0:n]
                      act(t1_ap, g_ap, AF.Tanh, [g_res], [t1], scale=0.5)
                      yield
                      stt('dve', t1_ap, t1_ap, 1.0, g_ap, ALU.add, ALU.mult, [t1, g_res], [t1])
                      yield
                      a_ = allocR()
                      stt('dve', a_[:, 0:n], t1_ap, 0.5, u_ap, ALU.mult, ALU.mult, [t1, u_res], [a_])
                      if not smp:
                          release(t1)
                          unhold(g_res)
                          unhold(u_res)
                      if u_sl is not None:
                          release(u_sl)
                      acts_d[j] = a_

                  js = list(range(h0, h1))
                  step = 1
                  for q0 in range(0, len(js), step):
                      gens = [ffn_gen(j) for j in js[q0:q0 + step]]
                      while gens:
                          for g_ in list(gens):
                              try:
                                  next(g_)
                              except StopIteration:
                                  gens.remove(g_)
                  acts = [acts_d[j] for j in js]
                  nh = h1 - h0
                  for m in range(8):
                      wd = wdring[wstate['d'] % 3]
                      wstate['d'] += 1
                      S.dma(wd[:, 0:nh, :], w_down[m, :, h0:h1, :], writes=[wd])
                      pb, pr = bank()
                      for jj in range(nh):
                          mm(pb[:, 0:n], wd[:, jj, :], acts[jj][:, 0:n], [wd, acts[jj]], [pr], start=(jj == 0), stop=(jj == nh - 1))
                      tt('dve', f(x1[m][:, 0:n]), f(x1[m][:, 0:n]), pb[:, 0:n], ALU.add, [x1[m], pr], [x1[m]])
                  release(*acts)
              release(*xn2)
              chk(6 + 10 * bi)
              yf = rmsnorm(x1, 2, n, False, inplace=True)
              for m in range(8):
                  S.dma(yT[m * 128:(m + 1) * 128, c0:c0 + n], f(yf[m][:, 0:n]), reads=[yf[m]])
              release(*yf)


        except _Stop:
            pass
        print("min free slots", mins)
        S.dma(s5o_d, s5o[:], reads=[s5o])
        S.dma(gco_d, gco[:], reads=[gco])
        S.dma(fco_d, fco[:], reads=[fco])
        S.finish()
    return nc


_CACHE = {}


def _consts():
    idx = np.arange(128)
    same = (idx[:, None] // 64) == (idx[None, :] // 64)
    ident = np.eye(128, dtype=np.float32)
    ones = np.ones((128, 128), np.float32)
    tri2 = (same & (idx[:, None] <= idx[None, :])).astype(np.float32)
    triC = (same & (idx[:, None] > idx[None, :])).astype(np.float32)
    mstrict = (same & (idx[:, None] < idx[None, :])).astype(np.float32)
    c = np.stack([ident, ones, tri2, triC, mstrict, ones], axis=1)
    maskB = np.zeros((128, 4, 128), np.float32)
    maskC = np.zeros((128, 4, 128), np.float32)
    for m in range(4):
        for r in range(128):
            for col in range(128):
                if r // 16 == 2 * m + col // 64:
                    maskB[r, m, col] = 1.0
                if r // 64 + 2 * m == col // 16:
                    maskC[r, m, col] = 1.0
    iota = np.tile(np.arange(SB, dtype=np.float32)[None, :], (128, 1))
    return np.ascontiguousarray(c), maskB, maskC, iota


def kernel(**inp):
    f32 = np.float32
    A = lambda a: np.ascontiguousarray(np.asarray(a, dtype=f32))
    ncores = 8
    if 'nc' not in _CACHE:
        _CACHE['nc'] = build()
    nc = _CACHE['nc']
    consts, maskB, maskC, iota = _consts()
    xp = A(inp['x_prompt'])
    xs = A(inp['x_sample'])
    chl = lambda v: A(np.asarray(v).reshape(-1).reshape(-1, 128).T)
    gains = A(np.stack([chl(inp['norm1_g'][0]), chl(inp['norm2_g'][0]), chl(inp['normf_g'])], axis=1))
    are, aim, ldt = A(inp['s5_a_re'][0]), A(inp['s5_a_im'][0]), A(inp['s5_log_dt'][0])
    ldt_gn = np.repeat(ldt[:, None], 64, axis=1)

    def bside(v):
        v4 = v.reshape(4, 8, 64)
        v5 = np.repeat(v4[:, :, None, :], 16, axis=2)
        return v5.reshape(4, 128, 64).transpose(1, 0, 2)
    s5bc = A(np.stack([bside(are), bside(aim), bside(ldt_gn)], axis=1))

    def bT(v):
        v = np.asarray(v).transpose(0, 2, 1).reshape(4, 8, 16, 64)
        return v.reshape(4, 128, 64).transpose(1, 0, 2)
    s5bT = A(np.stack([bT(inp['s5_b_re'][0]), bT(inp['s5_b_im'][0])], axis=1))
    s5ch = A(np.stack([chl(are), chl(aim), chl(ldt_gn)], axis=1))

    def cT(v):
        v = np.asarray(v).transpose(0, 2, 1).reshape(16, 128, 16)
        return v.transpose(1, 0, 2)
    s5cT = A(np.stack([cT(inp['s5_c_re'][0]), cT(inp['s5_c_im'][0])], axis=1))
    s5d = chl(inp['s5_d'][0])
    gcw = A(np.asarray(inp['gdn_conv_w'][0]).T.reshape(12, 128, 4).transpose(1, 0, 2))
    gsm = np.zeros((128, 9), f32)
    gsm[:, 0:4] = np.asarray(inp['gdn_a_log'][0])[None, :]
    gsm[:, 4:8] = np.asarray(inp['gdn_dt_bias'][0])[None, :]
    onorm = A(np.asarray(inp['gdn_onorm_g'][0]).reshape(128, 1))
    fcw = A(np.asarray(inp['ffn_conv_w'][0]).T.reshape(44, 128, 3).transpose(1, 0, 2))
    def tile_w(w, ncol_tiles):
        K, C = w.shape
        wp = np.zeros((K, ncol_tiles * 128), f32)
        wp[:, :C] = w
        return A(wp.reshape(K // 128, 128, ncol_tiles, 128).transpose(2, 1, 0, 3))
    w_in = tile_w(np.asarray(inp['w_in'][0]), 21)
    w_glu = tile_w(np.asarray(inp['s5_w_glu'][0]), 8)
    w_out = tile_w(np.asarray(inp['w_out'][0]), 8)
    wu_ = np.asarray(inp['ffn_w_up'][0])
    w_up = tile_w(wu_, 2 * NHT)
    w_down = A(np.asarray(inp['ffn_w_down'][0]).reshape(NHT, 128, 8, 128).transpose(2, 1, 0, 3))
    sre, sim_ = np.asarray(inp['state_s5_re'][0]), np.asarray(inp['state_s5_im'][0])
    sg, sgc, sfc = np.asarray(inp['state_gdn'][0]), np.asarray(inp['state_gdn_conv'][0]), np.asarray(inp['state_ffn_conv'][0])
    in_maps = []
    for c in range(ncores):
        sl = slice(c * NS_, (c + 1) * NS_)
        xT = np.concatenate([xp[c].T, xs[sl, 0].T], axis=1)

        def h0(v):
            return v[sl].reshape(NS_, 16, 128).transpose(2, 1, 0)
        m = dict(xT=A(xT), w_in=w_in, w_glu=w_glu, w_out=w_out, w_up=w_up, w_down=w_down, gains=gains,
                 s5bc=s5bc, s5bT=s5bT, s5ch=s5ch, s5cT=s5cT, s5d=s5d,
                 s5h0=A(np.stack([h0(sre), h0(sim_)], axis=1)), maskB=maskB, maskC=maskC, gcw=gcw, gsm=gsm,
                 onorm=onorm, gst=A(sg[sl]),
                 gcs=A(sgc[sl].transpose(2, 1, 0).reshape(12, 128, 3, NS_).transpose(1, 0, 2, 3)),
                 fcw=fcw, fcs=A(sfc[sl].transpose(2, 1, 0).reshape(44, 128, 2, NS_).transpose(1, 0, 2, 3)),
                 consts=consts, iota=iota)
        in_maps.append(m)
    res = run_bass_kernel_spmd(nc, in_maps, core_ids=list(range(ncores)))
    R = res.results
    y_p = np.stack([R[c]['yT'][:, :SEQ].T for c in range(ncores)], axis=0)
    y_s = np.concatenate([R[c]['yT'][:, SEQ:].T for c in range(ncores)], axis=0)[:, None, :]

    def s5st(c, ri, lo, hi):
        v = R[c]['s5o'][:, ri, :, lo:hi]
        return v.transpose(2, 1, 0).reshape(hi - lo, 32, 64)
    p_re = np.concatenate([s5st(c, 0, 0, 1) for c in range(ncores)], 0)[None]
    p_im = np.concatenate([s5st(c, 1, 0, 1) for c in range(ncores)], 0)[None]
    s_re = np.concatenate([s5st(c, 0, 1, 17) for c in range(ncores)], 0)[None]
    s_im = np.concatenate([s5st(c, 1, 1, 17) for c in range(ncores)], 0)[None]
    p_g = np.stack([R[c]['gdno'][0] for c in range(ncores)], 0)[None]
    s_g = np.concatenate([R[c]['gdno'][1:] for c in range(ncores)], 0)[None]

    def cv(c, key, nt, w, lo, hi):
        v = R[c][key][:, :, lo:hi, :]
        return v.transpose(2, 3, 1, 0).reshape(hi - lo, w, nt * 128)
    p_gc = np.concatenate([cv(c, 'gco', 12, 3, 0, 1) for c in range(ncores)], 0)[None]
    s_gc = np.concatenate([cv(c, 'gco', 12, 3, 1, 17) for c in range(ncores)], 0)[None]
    p_fc = np.concatenate([cv(c, 'fco', 44, 2, 0, 1) for c in range(ncores)], 0)[None]
    s_fc = np.concatenate([cv(c, 'fco', 44, 2, 1, 17) for c in range(ncores)], 0)[None]
    outs = (y_p, y_s, p_re, p_im, p_g, p_gc, p_fc, s_re, s_im, s_g, s_gc, s_fc)
    return tuple(np.ascontiguousarray(o.astype(f32)) for o in outs)
```
